# Optimizing a Trainium2 kernel written in Bass

```python
import math
import jax, jax.numpy as jnp
from jax import lax
import numpy as np

D_MODEL = 2048
BATCH = 1
SEQ = 8192
DEPTH = 4

N_META = 16
MIX_WIDTH = D_MODEL
V_DIM = 128
QK_NOPE = 128
QK_ROPE = 64
QK_DIM = QK_NOPE + QK_ROPE
ATT_HEADS = (MIX_WIDTH // 2) // V_DIM
ATT_WIDTH = ATT_HEADS * V_DIM
Q_LORA = 512
KV_LORA = 256
HY_WIDTH = MIX_WIDTH - ATT_WIDTH
HY_ORDER = 2
SHORT_CONV = 3
FILT_EMB = 33
FILT_BANDS = (FILT_EMB - 1) // 2
FILT_ORDER = 64
DECAY_TARGET = 1e-2
FAST_DECAY_PCT = 0.3
SLOW_DECAY_PCT = 1.5
D_FF = -(-8 * D_MODEL // (3 * 256)) * 256
ROPE_THETA = 10000.0
Q_BLOCK = 128
EPS = 1e-6
IN_COLS = Q_LORA + KV_LORA + QK_ROPE + (HY_ORDER + 1) * HY_WIDTH

kernel_name = 'hybrid_mla_hyena_encoder'


def rmsnorm(x, g):
    xf = x.astype(jnp.float32)
    y = xf * lax.rsqrt(jnp.mean(xf * xf, axis=-1, keepdims=True) + EPS)
    return (y * g.astype(jnp.float32)).astype(x.dtype)


def rope_tables(T):
    pos = jnp.arange(T, dtype=jnp.float32)
    inv = ROPE_THETA ** (-jnp.arange(0, QK_ROPE, 2, dtype=jnp.float32) / QK_ROPE)
    ang = pos[:, None] * inv[None, :]
    ang = jnp.concatenate([ang, ang], axis=-1)
    return jnp.cos(ang), jnp.sin(ang)


def apply_rope(x, cos, sin):
    half = QK_ROPE // 2
    x1, x2 = x[..., :half], x[..., half:]
    rot = jnp.concatenate([-x2, x1], axis=-1)
    c = cos[None, :, None, :].astype(x.dtype)
    s = sin[None, :, None, :].astype(x.dtype)
    return x * c + rot * s


def mla(c_q_raw, c_kv_raw, k_rope_raw, q_lat_g, kv_lat_g, w_uq, w_ukv, q_norm_g, k_norm_g, cos, sin):
    B, T, _ = c_q_raw.shape
    q = (rmsnorm(c_q_raw, q_lat_g) @ w_uq).reshape(B, T, ATT_HEADS, QK_DIM)
    kv = (rmsnorm(c_kv_raw, kv_lat_g) @ w_ukv).reshape(B, T, ATT_HEADS, QK_NOPE + V_DIM)
    k_nope, v = kv[..., :QK_NOPE], kv[..., QK_NOPE:]
    k_rope = jnp.broadcast_to(k_rope_raw[:, :, None, :], (B, T, ATT_HEADS, QK_ROPE))
    k = jnp.concatenate([k_nope, k_rope], axis=-1)
    q = rmsnorm(q, q_norm_g)
    k = rmsnorm(k, k_norm_g)
    q = jnp.concatenate([q[..., :QK_NOPE], apply_rope(q[..., QK_NOPE:], cos, sin)], axis=-1)
    k = jnp.concatenate([k[..., :QK_NOPE], apply_rope(k[..., QK_NOPE:], cos, sin)], axis=-1)
    scale = QK_DIM ** -0.5

    def attend(qb):
        s = jnp.einsum('bqhd,bkhd->bhqk', qb, k).astype(jnp.float32) * scale
        p = jax.nn.softmax(s, axis=-1).astype(v.dtype)
        return jnp.einsum('bhqk,bkhd->bqhd', p, v)

    o_meta = attend(q[:, :N_META])
    q_real = q[:, N_META:]
    nb = q_real.shape[1] // Q_BLOCK
    qb = q_real.reshape(B, nb, Q_BLOCK, ATT_HEADS, QK_DIM).transpose(1, 0, 2, 3, 4)
    o_real = lax.map(attend, qb)
    o_real = o_real.transpose(1, 0, 2, 3, 4).reshape(B, nb * Q_BLOCK, ATT_HEADS, V_DIM)
    o = jnp.concatenate([o_meta, o_real], axis=1)
    return o.reshape(B, T, ATT_WIDTH)


def short_conv(u, w, b):
    up = jnp.pad(u, ((0, 0), (1, 1), (0, 0)))
    return up[:, :-2] * w[0] + up[:, 1:-1] * w[1] + up[:, 2:] * w[2] + b


def implicit_filters(L, w1, b1, fr1, w2, b2, fr2, w3):
    f32 = jnp.float32
    t = jnp.linspace(0.0, 1.0, L, dtype=f32)[:, None]
    wpos = 2.0 * math.pi * jnp.arange(L, dtype=f32) / L
    freqs = jnp.linspace(1e-4, FILT_BANDS - 1, FILT_BANDS, dtype=f32)
    ang = wpos[:, None] * freqs[None, :]
    z = jnp.concatenate([t, jnp.cos(ang), -jnp.sin(ang)], axis=-1)
    h = jnp.sin(fr1.astype(f32) * (z @ w1.astype(f32) + b1.astype(f32)))
    h = jnp.sin(fr2.astype(f32) * (h @ w2.astype(f32) + b2.astype(f32)))
    h = h @ w3.astype(f32)
    max_decay = math.log(DECAY_TARGET) / FAST_DECAY_PCT
    min_decay = math.log(DECAY_TARGET) / SLOW_DECAY_PCT
    deltas = jnp.abs(jnp.linspace(min_decay, max_decay, HY_WIDTH, dtype=f32))
    decay = jnp.exp(-t * deltas[None, :])
    return h[:, :HY_WIDTH] * decay, h[:, HY_WIDTH:] * decay


def bidir_long_conv(u, h_f, h_b, d_skip):
    B, L, C = u.shape
    n = 2 * L
    kern = jnp.concatenate([h_f, jnp.zeros((1, C), jnp.float32), h_b[1:][::-1]], axis=0)
    U = jnp.fft.rfft(u.astype(jnp.float32), n=n, axis=1)
    K = jnp.fft.rfft(kern, n=n, axis=0)
    y = jnp.fft.irfft(U * K[None], n=n, axis=1)[:, :L]
    y = y + u.astype(jnp.float32) * d_skip.astype(jnp.float32)
    return y.astype(u.dtype)


def hyena(u_raw, conv_w, conv_b, w1, b1, fr1, w2, b2, fr2, w3, d_skip):
    u = short_conv(u_raw, conv_w, conv_b)
    x0 = u[..., :HY_WIDTH]
    x1 = u[..., HY_WIDTH:2 * HY_WIDTH]
    v = u[..., 2 * HY_WIDTH:]
    h_f, h_b = implicit_filters(u.shape[1], w1, b1, fr1, w2, b2, fr2, w3)
    v = bidir_long_conv(v * x1, h_f, h_b, d_skip)
    return v * x0


def setup_inputs(seed: int = 0) -> dict:
    key = jax.random.key(seed)
    ks = jax.random.split(key, 32)

    def nrm(k, shape, scale):
        return jax.random.normal(k, shape, jnp.float32) * scale

    return {
        'x': nrm(ks[0], (BATCH, SEQ, D_MODEL), 1.0),
        'meta_tokens': nrm(ks[1], (N_META, D_MODEL), 1.0),
        'norm_mix_g': 1.0 + nrm(ks[2], (DEPTH, D_MODEL), 0.02),
        'w_in': nrm(ks[3], (DEPTH, D_MODEL, IN_COLS), D_MODEL ** -0.5),
        'q_lat_g': 1.0 + nrm(ks[4], (DEPTH, Q_LORA), 0.02),
        'kv_lat_g': 1.0 + nrm(ks[5], (DEPTH, KV_LORA), 0.02),
        'w_uq': nrm(ks[6], (DEPTH, Q_LORA, ATT_HEADS * QK_DIM), Q_LORA ** -0.5),
        'w_ukv': nrm(ks[7], (DEPTH, KV_LORA, ATT_HEADS * (QK_NOPE + V_DIM)), KV_LORA ** -0.5),
        'q_norm_g': 1.0 + nrm(ks[8], (DEPTH, QK_DIM), 0.02),
        'k_norm_g': 1.0 + nrm(ks[9], (DEPTH, QK_DIM), 0.02),
        'conv_w': nrm(ks[10], (DEPTH, SHORT_CONV, (HY_ORDER + 1) * HY_WIDTH), SHORT_CONV ** -0.5),
        'conv_b': nrm(ks[11], (DEPTH, (HY_ORDER + 1) * HY_WIDTH), 0.02),
        'filt_w1': nrm(ks[12], (DEPTH, FILT_EMB, FILT_ORDER), FILT_EMB ** -0.5),
        'filt_b1': nrm(ks[13], (DEPTH, FILT_ORDER), 0.1),
        'filt_freq1': 1.0 + nrm(ks[14], (DEPTH, FILT_ORDER), 0.1),
        'filt_w2': nrm(ks[15], (DEPTH, FILT_ORDER, FILT_ORDER), FILT_ORDER ** -0.5),
        'filt_b2': nrm(ks[16], (DEPTH, FILT_ORDER), 0.1),
        'filt_freq2': 1.0 + nrm(ks[17], (DEPTH, FILT_ORDER), 0.1),
        'filt_w3': nrm(ks[18], (DEPTH, FILT_ORDER, 2 * HY_WIDTH), FILT_ORDER ** -0.5),
        'hy_skip': nrm(ks[19], (DEPTH, HY_WIDTH), 1.0),
        'attn_out_g': 1.0 + nrm(ks[20], (DEPTH, ATT_WIDTH), 0.02),
        'hy_out_g': 1.0 + nrm(ks[21], (DEPTH, HY_WIDTH), 0.02),
        'w_out': nrm(ks[22], (DEPTH, MIX_WIDTH, D_MODEL), MIX_WIDTH ** -0.5),
        'norm_ffn_g': 1.0 + nrm(ks[23], (DEPTH, D_MODEL), 0.02),
        'w_gate': nrm(ks[24], (DEPTH, D_MODEL, D_FF), D_MODEL ** -0.5),
        'w_up': nrm(ks[25], (DEPTH, D_MODEL, D_FF), D_MODEL ** -0.5),
        'w_down': nrm(ks[26], (DEPTH, D_FF, D_MODEL), D_FF ** -0.5),
    }


def reference(x, meta_tokens, norm_mix_g, w_in, q_lat_g, kv_lat_g, w_uq, w_ukv, q_norm_g, k_norm_g,
              conv_w, conv_b, filt_w1, filt_b1, filt_freq1, filt_w2, filt_b2, filt_freq2, filt_w3,
              hy_skip, attn_out_g, hy_out_g, w_out, norm_ffn_g, w_gate, w_up, w_down):
    B = x.shape[0]
    meta = jnp.broadcast_to(meta_tokens[None].astype(x.dtype), (B, N_META, D_MODEL))
    h = jnp.concatenate([meta, x], axis=1)
    T = h.shape[1]
    cos, sin = rope_tables(T)
    o1 = Q_LORA
    o2 = o1 + KV_LORA
    o3 = o2 + QK_ROPE
    for l in range(DEPTH):
        hn = rmsnorm(h, norm_mix_g[l])
        p = hn @ w_in[l]
        a = mla(p[..., :o1], p[..., o1:o2], p[..., o2:o3], q_lat_g[l], kv_lat_g[l],
                w_uq[l], w_ukv[l], q_norm_g[l], k_norm_g[l], cos, sin)
        y = hyena(p[..., o3:], conv_w[l], conv_b[l], filt_w1[l], filt_b1[l], filt_freq1[l],
                  filt_w2[l], filt_b2[l], filt_freq2[l], filt_w3[l], hy_skip[l])
        mix = jnp.concatenate([rmsnorm(a, attn_out_g[l]), rmsnorm(y, hy_out_g[l])], axis=-1)
        h = h + mix @ w_out[l]
        hn = rmsnorm(h, norm_ffn_g[l])
        h = h + (jax.nn.silu(hn @ w_gate[l]) * (hn @ w_up[l])) @ w_down[l]
    return h[:, N_META:]
```

```python
import numpy as np
import ml_dtypes
from contextlib import ExitStack
import concourse.bass as bass
import concourse.mybir as mybir
from concourse.bass_utils import run_bass_kernel_spmd

F32 = mybir.dt.float32
BF16 = mybir.dt.bfloat16
AF = mybir.ActivationFunctionType
ALU = mybir.AluOpType
ENGS = ('sync', 'scalar', 'vector', 'gpsimd', 'tensor')

NCORE = 8
D = 2048
SEQ = 8192
NMETA = 16
T = SEQ + NMETA
TC = T // NCORE
CH = [(0, 342), (342, 342), (684, 342)]
DEPTH = 4
NH = 8
QLORA = 512
KVLORA = 256
ROPE = 64
NOPE = 128
QK = 192
VD = 128
HYW = 1024
DFF = 5632
INCOLS = 3904
EPS = 1e-6
FEMB = 33
FORD = 64
NBLK = 65
KW = 16639
KPAD = 112


class Dep:
    __slots__ = ('w', 'r')

    def __init__(s):
        s.w = None
        s.r = {}


class SP:
    __slots__ = ('sem', 'idx', 'know')

    def __init__(s, sem, idx, know):
        s.sem = sem
        s.idx = idx
        s.know = know


def alias(olds):
    d = Dep()
    for o in olds:
        sps = list(o.r.values())
        if o.w is not None:
            sps.append(o.w)
        for sp in sps:
            if sp.sem not in d.r or d.r[sp.sem].idx < sp.idx:
                d.r[sp.sem] = sp
    return d


class Prog:
    def __init__(self):
        self.nc = bass.Bass("TRN2", target_bir_lowering=False)
        self.ops = {e: [] for e in ENGS}
        self.know = {e: {} for e in ENGS}
        self.cnt = {}
        self.targets = {}
        self.nps = 0
        self.out_deps = []

    def emit(self, eng, fn, reads=(), writes=(), dsem=None, strict=False):
        is_dma = dsem is not None
        own = 'E_' + eng
        sem = dsem if is_dma else own
        know = self.know[eng]
        deps = []
        for d in reads:
            if d.w is not None:
                deps.append(d.w)
        for d in writes:
            if d.w is not None:
                deps.append(d.w)
            deps.extend(d.r.values())
        waits = {}
        for sp in deps:
            if (not is_dma) and sp.sem == own and not strict:
                continue
            if know.get(sp.sem, 0) >= sp.idx:
                continue
            waits[sp.sem] = max(waits.get(sp.sem, 0), sp.idx)
            for s, v in sp.know.items():
                if know.get(s, 0) < v:
                    know[s] = v
        for s, v in waits.items():
            self.targets.setdefault(s, set()).add(v)
        if fn is None:
            self.ops[eng].append((None, waits, None, 0))
            return
        idx = self.cnt.get(sem, 0) + 1
        self.cnt[sem] = idx
        snap = dict(know)
        snap[sem] = idx
        sp = SP(sem, idx, snap)
        for d in reads:
            if sem not in d.r or d.r[sem].idx < idx:
                d.r[sem] = sp
        for d in writes:
            d.w = sp
            d.r = {}
        self.ops[eng].append((fn, waits, sem, idx))

    def finish(self):
        self.emit('sync', None, reads=self.out_deps)

    def build(self):
        nc = self.nc
        rank = {}
        for s in self.cnt:
            if s.startswith('E_'):
                t = sorted(self.targets.get(s, ()))
                rank[s] = {v: i + 1 for i, v in enumerate(t)}
        with ExitStack() as st:
            sems = {s: st.enter_context(nc.semaphore(s)) for s in self.cnt}
            block = st.enter_context(nc.Block())

            def run(e, eng):
                for fn, waits, sem, idx in self.ops[eng]:
                    for s, v in waits.items():
                        val = rank[s][v] if s.startswith('E_') else 16 * v
                        e.wait_ge(sems[s], val)
                    if fn is None:
                        continue
                    ins = fn(e)
                    if sem.startswith('E_'):
                        if idx in rank[sem]:
                            ins.then_inc(sems[sem], 1)
                    else:
                        ins.then_inc(sems[sem], 16)

            @block.sync
            def _(e):
                run(e, 'sync')

            @block.scalar
            def _(e):
                run(e, 'scalar')

            @block.vector
            def _(e):
                run(e, 'vector')

            @block.gpsimd
            def _(e):
                run(e, 'gpsimd')

            @block.tensor
            def _(e):
                run(e, 'tensor')
        return nc

    def mm(self, ps, lhsT, rhs, start, stop, rd, psd):
        self.emit('tensor', lambda e: e.matmul(ps, lhsT=lhsT, rhs=rhs, start=start, stop=stop),
                  reads=rd, writes=[psd])

    def act(self, out, in_, func, rd, wr, scale=1.0, bias=None):
        if bias is None:
            self.emit('scalar', lambda e: e.activation(out=out, in_=in_, func=func, scale=scale),
                      reads=rd, writes=wr)
        else:
            self.emit('scalar', lambda e: e.activation(out=out, in_=in_, func=func, scale=scale, bias=bias),
                      reads=rd, writes=wr)

    def copy(self, eng, out, in_, rd, wr):
        if eng == 'scalar':
            self.emit('scalar', lambda e: e.copy(out=out, in_=in_), reads=rd, writes=wr)
        else:
            self.emit(eng, lambda e: e.tensor_copy(out=out, in_=in_), reads=rd, writes=wr)

    def tt(self, eng, out, in0, in1, op, rd, wr):
        self.emit(eng, lambda e: e.tensor_tensor(out=out, in0=in0, in1=in1, op=op), reads=rd, writes=wr)

    def ts(self, eng, out, in0, s1, s2, op0, op1, rd, wr, strict=False):
        if s2 is None:
            self.emit(eng, lambda e: e.tensor_scalar(out=out, in0=in0, scalar1=s1, scalar2=None, op0=op0),
                      reads=rd, writes=wr, strict=strict)
        else:
            self.emit(eng, lambda e: e.tensor_scalar(out=out, in0=in0, scalar1=s1, scalar2=s2, op0=op0, op1=op1),
                      reads=rd, writes=wr, strict=strict)

    def stt(self, eng, out, in0, scalar, in1, op0, op1, rd, wr):
        self.emit(eng, lambda e: e.scalar_tensor_tensor(out=out, in0=in0, scalar=scalar, in1=in1, op0=op0, op1=op1),
                  reads=rd, writes=wr)

    def recip(self, out, in_, rd, wr):
        self.emit('vector', lambda e: e.reciprocal(out=out, in_=in_), reads=rd, writes=wr)

    def memset(self, eng, ap, val, wr):
        self.emit(eng, lambda e: e.memset(ap, val), writes=wr)

    def dma(self, eng, out, in_, rd, wr, sem, is_out=False):
        self.emit(eng, lambda e: e.dma_start(out=out, in_=in_), reads=rd, writes=wr, dsem=sem)


class Arena:
    def __init__(s, nc, cols=47600):
        s.t = nc.alloc_sbuf_tensor("arena", [128, cols], F32)
        s.off = 0
        s.cols = cols

    def f32(s, n, rows=128):
        a = s.t[0:rows, s.off:s.off + n]
        s.off += n
        assert s.off <= s.cols, (s.off, s.cols)
        return a

    def bf16(s, n, rows=128):
        m = (n + 1) // 2
        a = s.t[0:rows, s.off:s.off + m].bitcast(BF16)
        s.off += m
        assert s.off <= s.cols, (s.off, s.cols)
        return a

    def mark(s):
        return s.off

    def reset(s, m):
        s.off = m


class Ctx:
    def __init__(self):
        self.p = Prog()
        self.nc = self.p.nc
        self.ar = Arena(self.nc)
        self.banks = []
        for i in range(8):
            t = self.nc.alloc_psum_tensor("psb%d" % i, [128, 512], F32)
            self.banks.append((t, Dep()))
        self.bi = 0
        self.dq = 0
        self.ev = 0

    def psum(self):
        b = self.banks[self.bi % 8]
        self.bi += 1
        return b

    def din(self, name, shape, dt=F32):
        return self.nc.dram_tensor(name, list(shape), dt, kind="ExternalInput")

    def dout(self, name, shape, dt=F32):
        return self.nc.dram_tensor(name, list(shape), dt, kind="ExternalOutput")

    def load_consts(self):
        c = self.din('cst', [128, 128 * 3 + 64])
        self.cst = self.ar.f32(128 * 3 + 64)
        self.cst_d = Dep()
        self.p.dma('sync', self.cst, c.ap(), [], [self.cst_d], 'd_cst')
        self.ones = self.cst[:, 0:128]
        self.ident = self.cst[:, 128:256]
        self.J = self.cst[:, 256:384]
        self.R = self.cst[0:64, 384:448]
        self.epsc = self.ar.f32(1)
        self.p.memset('vector', self.epsc, EPS, [self.cst_d])

    def evq(self):
        self.ev += 1
        return 'scalar' if self.ev % 2 else 'vector'


def host_consts():
    c = np.zeros((128, 448), np.float32)
    c[:, 0:128] = 1.0
    c[:, 128:256] = np.eye(128, dtype=np.float32)
    c[:, 256:384] = np.eye(128, dtype=np.float32)[::-1]
    Rm = np.zeros((64, 64), np.float32)
    for m in range(32):
        Rm[m + 32, m] = -1.0
        Rm[m, m + 32] = 1.0
    c[0:64, 384:448] = Rm
    return c


def rms_rstd(cx, srcs, nfeat, width):
    p = cx.p
    ps, psd = cx.psum()
    n = len(srcs)
    for i, (ap, deps, rows) in enumerate(srcs):
        sq = cx.sq[cx.sqi % 2]
        sqd = cx.sqd[cx.sqi % 2]
        cx.sqi += 1
        p.act(sq[0:rows, 0:width], ap, AF.Square, deps, [sqd])
        p.mm(ps[:, 0:width], cx.ones[0:rows, :], sq[0:rows, 0:width], i == 0, i == n - 1, [sqd, cx.cst_d], psd)
    r = cx.rs[cx.rsi % 2]
    rd = cx.rsd[cx.rsi % 2]
    cx.rsi += 1
    p.act(r[:, 0:width], ps[:, 0:width], AF.Sqrt, [psd, cx.cst_d], [rd], scale=1.0 / nfeat, bias=cx.epsc[:, 0:1])
    p.recip(r[:, 0:width], r[:, 0:width], [rd], [rd])
    return r, rd


def alloc_rms_scratch(cx):
    cx.sq = [cx.ar.f32(342), cx.ar.f32(342)]
    cx.sqd = [Dep(), Dep()]
    cx.sqi = 0
    cx.rs = [cx.ar.f32(342), cx.ar.f32(342)]
    cx.rsd = [Dep(), Dep()]
    cx.rsi = 0


def linear(cx, xs, w, K, m_chunks, mblk, evac, wslots):
    p = cx.p
    KC = len(xs)
    blocks = []
    cur = []
    cur0 = None
    for (m0, msz) in m_chunks:
        if cur and (m0 + msz - cur0 > mblk or m0 != cur[-1][0] + cur[-1][1]):
            blocks.append((cur0, cur))
            cur = []
        if not cur:
            cur0 = m0
        cur.append((m0, msz))
    if cur:
        blocks.append((cur0, cur))
    for (b0, chunks) in blocks:
        bw = chunks[-1][0] + chunks[-1][1] - b0
        slot = wslots[cx.wi % len(wslots)]
        cx.wi += 1
        wap, wd, wsem = slot
        wv = wap[:, 0:KC * mblk].rearrange("p (k m) -> p k m", m=mblk)
        rows_last = xs[-1][2]
        if rows_last == 128:
            src = w[0:KC * 128, b0:b0 + bw].rearrange("(k p) m -> p k m", p=128)
            p.dma('gpsimd', wv[:, :, 0:bw], src, [], [wd], wsem)
        else:
            if KC > 1:
                src = w[0:(KC - 1) * 128, b0:b0 + bw].rearrange("(k p) m -> p k m", p=128)
                p.dma('gpsimd', wv[:, 0:KC - 1, 0:bw], src, [], [wd], wsem)
            src = w[(KC - 1) * 128:(KC - 1) * 128 + rows_last, b0:b0 + bw]
            p.dma('gpsimd', wv[0:rows_last, KC - 1, 0:bw], src, [], [wd], wsem)
        for (m0, msz) in chunks:
            for ci, (t0, tn) in enumerate(CH):
                ps, psd = cx.psum()
                for kc, (xap, xd, rows) in enumerate(xs):
                    p.mm(ps[0:msz, 0:tn], wv[0:rows, kc, m0 - b0:m0 - b0 + msz], xap[0:rows, t0:t0 + tn],
                         kc == 0, kc == KC - 1, [wd, xd], psd)
                evac(m0, msz, ci, t0, tn, ps, psd)


def build_A():
    cx = Ctx()
    p, nc, ar = cx.p, cx.nc, cx.ar
    hT = cx.din('hT', [D, TC])
    w_in = cx.din('w_in', [D, INCOLS])
    w_uq = cx.din('w_uq', [QLORA, NH * QK])
    w_ukv = cx.din('w_ukv', [KVLORA, NH * (NOPE + VD)])
    gv = cx.din('gv', [128, 26])
    cs = cx.din('cs', [64, 2 * TC])
    qT_o = cx.dout('qT', [NH, QK, TC], BF16)
    kT_o = cx.dout('kT', [NH, QK, TC], BF16)
    v_o = cx.dout('v', [TC, NH * VD], BF16)
    uT_o = cx.dout('uT', [3 * HYW, TC])
    cx.load_consts()
    alloc_rms_scratch(cx)
    cx.wi = 0
    g = ar.f32(26)
    gd = Dep()
    p.dma('sync', g, gv.ap(), [], [gd], 'd_g')
    cst = ar.f32(2 * TC, rows=64)
    csd = Dep()
    p.dma('sync', cst, cs.ap(), [], [csd], 'd_cs')
    cos = cst[:, 0:TC]
    sin = cst[:, TC:2 * TC]
    wslots = [(ar.bf16(16 * 512), Dep(), 'd_w%d' % i) for i in range(2)]
    cq = [(ar.f32(TC), Dep()) for _ in range(4)]
    ckv = [(ar.f32(TC), Dep()) for _ in range(2)]
    kr = (ar.f32(TC, rows=64), Dep())
    stage = [(ar.f32(TC), Dep(), 'd_st%d' % i) for i in range(2)]
    hn = [(ar.bf16(TC), Dep()) for _ in range(16)]
    mk = ar.mark()
    h = [(ar.f32(TC), Dep()) for _ in range(16)]
    for kc in range(16):
        p.dma('sync', h[kc][0], hT.ap()[kc * 128:(kc + 1) * 128, :], [], [h[kc][1]], 'd_h%d' % (kc % 4))
    for (t0, tn) in CH:
        r, rd = rms_rstd(cx, [(h[kc][0][:, t0:t0 + tn], [h[kc][1]], 128) for kc in range(16)], D, tn)
        for kc in range(16):
            p.stt('vector', hn[kc][0][:, t0:t0 + tn], h[kc][0][:, t0:t0 + tn], g[:, kc:kc + 1],
                  r[:, 0:tn], ALU.mult, ALU.mult, [h[kc][1], gd, rd], [hn[kc][1]])
    sti = [0]

    def evac_in(m0, msz, ci, t0, tn, ps, psd):
        if m0 < 512:
            dst, dd = cq[m0 // 128]
            p.copy(cx.evq(), dst[:, t0:t0 + tn], ps[0:msz, 0:tn], [psd], [dd])
        elif m0 < 768:
            dst, dd = ckv[(m0 - 512) // 128]
            p.copy(cx.evq(), dst[:, t0:t0 + tn], ps[0:msz, 0:tn], [psd], [dd])
        elif m0 < 832:
            p.copy(cx.evq(), kr[0][:, t0:t0 + tn], ps[0:64, 0:tn], [psd], [kr[1]])
        else:
            sap, sd, ssem = stage[sti[0] % 2]
            p.copy(cx.evq(), sap[:, t0:t0 + tn], ps[0:msz, 0:tn], [psd], [sd])
            if ci == len(CH) - 1:
                od = Dep()
                p.dma('sync', uT_o.ap()[m0 - 832:m0 - 832 + 128, :], sap, [sd], [od], ssem)
                p.out_deps.append(od)
                sti[0] += 1

    m_chunks = [(i * 128, 128) for i in range(6)] + [(768, 64)] + [(832 + i * 128, 128) for i in range(24)]
    xs = [(hn[kc][0], hn[kc][1], 128) for kc in range(16)]
    linear(cx, xs, w_in.ap(), D, m_chunks, 512, evac_in, wslots)
    ar.reset(mk)
    hdeps = [x[1] for x in h]
    ad = alias(hdeps)

    def al():
        return alias(hdeps)

    cqn = [(ar.bf16(TC), al()) for _ in range(4)]
    ckvn = [(ar.bf16(TC), al()) for _ in range(2)]
    for (t0, tn) in CH:
        r, rd = rms_rstd(cx, [(cq[i][0][:, t0:t0 + tn], [cq[i][1]], 128) for i in range(4)], QLORA, tn)
        for i in range(4):
            p.stt('vector', cqn[i][0][:, t0:t0 + tn], cq[i][0][:, t0:t0 + tn], g[:, 16 + i:17 + i], r[:, 0:tn],
                  ALU.mult, ALU.mult, [cq[i][1], gd, rd], [cqn[i][1]])
        r, rd = rms_rstd(cx, [(ckv[i][0][:, t0:t0 + tn], [ckv[i][1]], 128) for i in range(2)], KVLORA, tn)
        for i in range(2):
            p.stt('vector', ckvn[i][0][:, t0:t0 + tn], ckv[i][0][:, t0:t0 + tn], g[:, 20 + i:21 + i], r[:, 0:tn],
                  ALU.mult, ALU.mult, [ckv[i][1], gd, rd], [ckvn[i][1]])
    hold = [(ar.f32(342), al()) for _ in range(2)]
    holdr = [(ar.f32(342, rows=64), al()) for _ in range(2)]
    rot = [(ar.f32(342, rows=64), al()) for _ in range(2)]
    tmpc = [(ar.f32(342, rows=64), al()) for _ in range(2)]
    outn = [(ar.bf16(TC), al(), 'd_on%d' % i) for i in range(2)]
    outr = [(ar.bf16(TC, rows=64), al(), 'd_or%d' % i) for i in range(2)]
    krsq = (ar.f32(TC, rows=64), al())
    state = {'i': 0, 'o': 0}

    def qk_head(which, hd, dst, gcol):
        pass

    wq_slots = wslots
    def head_pass(which):
        KC = 4 if which == 'q' else 2
        xs_ = cqn if which == 'q' else ckvn
        wdr = w_uq.ap() if which == 'q' else w_ukv.ap()
        gn = 22 if which == 'q' else 24
        out_dram = qT_o if which == 'q' else kT_o
        for hh in range(NH):
            wap, wd, wsem = wslots[cx.wi % 2]
            cx.wi += 1
            ncol = 192 if which == 'q' else 128
            c0 = hh * 192 if which == 'q' else hh * 256
            wv = wap[:, 0:KC * 192].rearrange("p (k m) -> p k m", m=192)
            p.dma('gpsimd', wv[:, :, 0:ncol], wdr[:, c0:c0 + ncol].rearrange("(k p) m -> p k m", p=128),
                  [], [wd], wsem)
            on, ond, onsem = outn[state['o'] % 2]
            orr, ord_, orsem = outr[state['o'] % 2]
            state['o'] += 1
            for (t0, tn) in CH:
                i = state['i'] % 2
                state['i'] += 1
                psn, psnd = cx.psum()
                for kc in range(KC):
                    p.mm(psn[:, 0:tn], wv[:, kc, 0:128], xs_[kc][0][:, t0:t0 + tn], kc == 0, kc == KC - 1,
                         [wd, xs_[kc][1]], psnd)
                srcs = [(psn[:, 0:tn], [psnd], 128)]
                if which == 'q':
                    psr, psrd = cx.psum()
                    for kc in range(KC):
                        p.mm(psr[0:64, 0:tn], wv[:, kc, 128:192], xs_[kc][0][:, t0:t0 + tn], kc == 0, kc == KC - 1,
                             [wd, xs_[kc][1]], psrd)
                    rsrc, rsd = psr[0:64, 0:tn], psrd
                else:
                    rsrc, rsd = kr[0][:, t0:t0 + tn], kr[1]
                srcs.append((rsrc, [rsd], 64))
                r, rd = rms_rstd(cx, srcs, QK, tn)
                p.stt('vector', on[:, t0:t0 + tn], psn[:, 0:tn], g[:, gn:gn + 1], r[:, 0:tn], ALU.mult, ALU.mult,
                      [psnd, gd, rd], [ond])
                hr, hrd = holdr[i]
                p.stt('vector', hr[:, 0:tn], rsrc, g[0:64, gn + 1:gn + 2], r[0:64, 0:tn], ALU.mult, ALU.mult,
                      [rsd, gd, rd], [hrd])
                pr, prd = cx.psum()
                p.mm(pr[0:64, 0:tn], cx.R, hr[:, 0:tn], True, True, [hrd, cx.cst_d], prd)
                tc_, tcd = tmpc[i]
                p.tt('vector', tc_[:, 0:tn], pr[0:64, 0:tn], sin[:, t0:t0 + tn], ALU.mult, [prd, csd], [tcd])
                p.tt('gpsimd', hr[:, 0:tn], hr[:, 0:tn], cos[:, t0:t0 + tn], ALU.mult, [hrd, csd], [hrd])
                p.tt('vector', orr[:, t0:t0 + tn], hr[:, 0:tn], tc_[:, 0:tn], ALU.add, [hrd, tcd], [ord_])
            od = Dep()
            p.dma('sync', out_dram.ap()[hh, 0:128, :], on, [ond], [od], onsem)
            p.out_deps.append(od)
            od = Dep()
            p.dma('sync', out_dram.ap()[hh, 128:192, :], orr, [ord_], [od], orsem)
            p.out_deps.append(od)

    head_pass('q')
    head_pass('k')
    wvv = [(ar.bf16(2 * 512), al(), 'd_wv%d' % i) for i in range(2)]
    vst = [(ar.bf16(512), al(), 'd_vs%d' % i) for i in range(2)]
    vi = 0
    for half in range(2):
        wap, wd, wsem = wvv[half]
        wv = wap.rearrange("p (k h m) -> p k h m", k=2, h=4)
        for hh in range(4):
            c0 = (half * 4 + hh) * 256 + 128
            p.dma('gpsimd', wv[:, :, hh, :], w_ukv.ap()[:, c0:c0 + 128].rearrange("(k p) m -> p k m", p=128),
                  [], [wd], wsem)
        wflat = wap.rearrange("p (k n) -> p k n", k=2)
        for tb in range(9):
            t0 = tb * 128
            tn = min(128, TC - t0)
            ps, psd = cx.psum()
            for kc in range(2):
                p.mm(ps[0:tn, 0:512], ckvn[kc][0][:, t0:t0 + tn], wflat[:, kc, :], kc == 0, kc == 1,
                     [wd, ckvn[kc][1]], psd)
            sap, sd, ssem = vst[vi % 2]
            vi += 1
            p.copy(cx.evq(), sap[0:tn, :], ps[0:tn, 0:512], [psd], [sd])
            od = Dep()
            p.dma('sync', v_o.ap()[t0:t0 + tn, half * 512:(half + 1) * 512], sap[0:tn, :], [sd], [od], ssem)
            p.out_deps.append(od)
    p.finish()
    return p.build()


def pack_gains_A(norm_g, q_lat_g, kv_lat_g, q_norm_g, k_norm_g):
    g = np.zeros((128, 26), np.float32)
    g[:, 0:16] = norm_g.reshape(16, 128).T
    g[:, 16:20] = q_lat_g.reshape(4, 128).T
    g[:, 20:22] = kv_lat_g.reshape(2, 128).T
    g[:, 22] = q_norm_g[0:128]
    g[0:64, 23] = q_norm_g[128:192]
    g[:, 24] = k_norm_g[0:128]
    g[0:64, 25] = k_norm_g[128:192]
    return g


def rope_tables_np():
    pos = np.arange(T, dtype=np.float32)
    inv = (np.float32(10000.0) ** (-np.arange(0, ROPE, 2, dtype=np.float32) / np.float32(ROPE))).astype(np.float32)
    ang = (pos[:, None] * inv[None, :]).astype(np.float32)
    ang = np.concatenate([ang, ang], axis=-1)
    return np.cos(ang).astype(np.float32), np.sin(ang).astype(np.float32)


def build_B():
    cx = Ctx()
    p, nc, ar = cx.p, cx.nc, cx.ar
    qT = cx.din('qT', [QK, T], BF16)
    kT = cx.din('kT', [QK, T], BF16)
    vin = cx.din('v', [T, VD], BF16)
    uT = cx.din('uT', [3, 128, T])
    hp = cx.din('hp', [128, 13])
    fw1 = cx.din('fw1', [FEMB, FORD])
    fw2 = cx.din('fw2', [FORD, FORD])
    fw3 = cx.din('fw3', [FORD, 256])
    fpp = cx.din('fpp', [FORD, 4])
    zt = cx.din('zt', [FEMB, 2 * T])
    dec = cx.din('dec', [128, 2 * T])
    a_o = cx.dout('a', [T, VD])
    y_o = cx.dout('yT', [128, T])
    krev = nc.dram_tensor('krev', [128, KW], BF16, kind="Internal")
    cx.load_consts()
    mk0 = ar.mark()
    qn = (ar.bf16(T), Dep())
    qr = (ar.bf16(T, rows=64), Dep())
    kn = (ar.bf16(T), Dep())
    kr = (ar.bf16(T, rows=64), Dep())
    p.dma('sync', qn[0], qT.ap()[0:128, :], [], [qn[1]], 'd_q0')
    p.dma('sync', qr[0], qT.ap()[128:192, :], [], [qr[1]], 'd_q1')
    p.dma('gpsimd', kn[0], kT.ap()[0:128, :], [], [kn[1]], 'd_k0')
    p.dma('gpsimd', kr[0], kT.ap()[128:192, :], [], [kr[1]], 'd_k1')
    Vt = ar.bf16(NBLK * 130)
    Vd = Dep()
    V3 = Vt[:, 0:NBLK * 130].rearrange("p (k d) -> p k d", d=130)
    p.memset('vector', Vt, 1.0, [Vd])
    p.dma('sync', V3[:, 0:64, 0:128], vin.ap()[0:8192, :].rearrange("(k p) d -> p k d", p=128), [], [Vd], 'd_v')
    p.dma('sync', V3[0:16, 64, 0:128], vin.ap()[8192:T, :], [], [Vd], 'd_v')
    pts = [(ar.bf16(512), Dep()) for _ in range(4)]
    rcs = [(ar.f32(4), Dep()) for _ in range(2)]
    osb = [(ar.f32(512), Dep(), 'd_os%d' % i) for i in range(2)]
    sbk = cx.banks[0:4]
    obk = cx.banks[4:8]
    si = 0
    pi = 0
    scale = float(QK) ** -0.5
    att_deps = [qn[1], qr[1], kn[1], kr[1], Vd] + [x[1] for x in pts] + [x[1] for x in rcs] + [x[1] for x in osb]
    for qc in range(17):
        q0 = qc * 512
        qsz = min(512, T - q0)
        nsub = (qsz + 127) // 128
        for kt in range(NBLK):
            k0 = kt * 128
            ksz = min(128, T - k0)
            ps, psd = sbk[si % 4]
            si += 1
            p.mm(ps[0:ksz, 0:qsz], kn[0][:, k0:k0 + ksz], qn[0][:, q0:q0 + qsz], True, False, [kn[1], qn[1]], psd)
            p.mm(ps[0:ksz, 0:qsz], kr[0][:, k0:k0 + ksz], qr[0][:, q0:q0 + qsz], False, True, [kr[1], qr[1]], psd)
            pt, ptd = pts[pi % 4]
            pi += 1
            p.act(pt[0:ksz, 0:qsz], ps[0:ksz, 0:qsz], AF.Exp, [psd], [ptd], scale=scale)
            for sub in range(nsub):
                s0 = sub * 128
                ssz = min(128, qsz - s0)
                p.mm(obk[sub][0][0:ssz, 0:129], pt[0:ksz, s0:s0 + ssz], V3[0:ksz, kt, 0:129], kt == 0, kt == NBLK - 1,
                     [ptd, Vd], obk[sub][1])
        rc, rcd = rcs[qc % 2]
        ob_, obd, osem = osb[qc % 2]
        ob3 = ob_.rearrange("p (s d) -> p s d", d=128)
        for sub in range(nsub):
            s0 = sub * 128
            ssz = min(128, qsz - s0)
            p.recip(rc[0:ssz, sub:sub + 1], obk[sub][0][0:ssz, 128:129], [obk[sub][1]], [rcd])
        for sub in range(nsub):
            s0 = sub * 128
            ssz = min(128, qsz - s0)
            p.ts('vector', ob3[0:ssz, sub, :], obk[sub][0][0:ssz, 0:128], rc[0:ssz, sub:sub + 1], None, ALU.mult, None,
                 [obk[sub][1], rcd], [obd], strict=True)
        od = Dep()
        if qsz == 512:
            p.dma('sync', a_o.ap()[q0:q0 + 512, :].rearrange("(s p) d -> p s d", p=128), ob3, [obd], [od], osem)
        else:
            p.dma('sync', a_o.ap()[q0:q0 + qsz, :], ob3[0:qsz, 0, :], [obd], [od], osem)
        p.out_deps.append(od)
    ar.reset(mk0)

    def al():
        return alias(att_deps)

    hpt = (ar.f32(13), al())
    p.dma('sync', hpt[0], hp.ap(), [], [hpt[1]], 'd_hp')
    w1t = (ar.f32(FORD, rows=FEMB), al())
    w2t = (ar.f32(FORD, rows=FORD), al())
    w3t = (ar.f32(256, rows=FORD), al())
    fpt = (ar.f32(8, rows=FORD), al())
    p.dma('sync', w1t[0], fw1.ap(), [], [w1t[1]], 'd_f1')
    p.dma('sync', w2t[0], fw2.ap(), [], [w2t[1]], 'd_f2')
    p.dma('sync', w3t[0], fw3.ap(), [], [w3t[1]], 'd_f3')
    p.dma('sync', fpt[0][:, 0:4], fpp.ap(), [], [fpt[1]], 'd_f4')
    p.tt('vector', fpt[0][:, 4:5], fpt[0][:, 0:1], fpt[0][:, 1:2], ALU.mult, [fpt[1]], [fpt[1]])
    p.tt('vector', fpt[0][:, 5:6], fpt[0][:, 2:3], fpt[0][:, 3:4], ALU.mult, [fpt[1]], [fpt[1]])
    p.memset('vector', fpt[0][:, 6:7], -float(np.pi), [fpt[1]])
    zc = [(ar.f32(512, rows=FEMB), al(), 'd_zc%d' % i) for i in range(2)]
    dc = [(ar.f32(512), al(), 'd_dc%d' % i) for i in range(2)]
    a1 = [(ar.f32(512, rows=FORD), al()) for _ in range(2)]
    a2 = [(ar.f32(512, rows=FORD), al()) for _ in range(2)]
    kb = [(ar.bf16(512), al(), 'd_kb%d' % i) for i in range(2)]
    zz = (ar.bf16(KPAD), al())
    p.memset('vector', zz[0], 0.0, [zz[1]])
    krds = []
    d0 = Dep()
    p.dma('sync', krev.ap()[:, 0:KPAD], zz[0], [zz[1]], [d0], 'd_kz')
    krds.append(d0)
    d0 = Dep()
    p.dma('sync', krev.ap()[:, KW - KPAD:KW], zz[0], [zz[1]], [d0], 'd_kz')
    krds.append(d0)
    TWO_PI = float(2 * np.pi)
    PI_S = 3.1415925
    rr_i = (ar.f32(512, rows=FORD).bitcast(mybir.dt.int32), al())
    rr_f = (ar.f32(512, rows=FORD), al())

    def sin_reduced(a, ad, n):
        p.ts('vector', rr_i[0][:, 0:n], a, 1.0 / TWO_PI, None, ALU.mult, None, [ad], [rr_i[1]])
        p.copy('vector', rr_f[0][:, 0:n], rr_i[0][:, 0:n], [rr_i[1]], [rr_f[1]])
        p.stt('vector', a, rr_f[0][:, 0:n], -TWO_PI, a, ALU.mult, ALU.add, [rr_f[1], ad], [ad])
        p.ts('vector', a, a, -PI_S, PI_S, ALU.max, ALU.min, [ad], [ad])
        p.act(a, a, AF.Sin, [ad], [ad])
    it = 0
    for ps_ in range(2):
        for c in range(17):
            c0 = c * 512
            n = min(512, T - c0)
            zt_, zd, zsem = zc[it % 2]
            dc_, dd, dsem = dc[it % 2]
            a1_, a1d = a1[it % 2]
            a2_, a2d = a2[it % 2]
            kb_, kbd, ksem = kb[it % 2]
            it += 1
            p.dma('sync', zt_[:, 0:n], zt.ap()[:, ps_ * T + c0:ps_ * T + c0 + n], [], [zd], zsem)
            p.dma('sync', dc_[:, 0:n], dec.ap()[:, ps_ * T + c0:ps_ * T + c0 + n], [], [dd], dsem)
            ps1, ps1d = cx.psum()
            p.mm(ps1[0:FORD, 0:n], w1t[0], zt_[:, 0:n], True, True, [w1t[1], zd], ps1d)
            p.ts('vector', a1_[:, 0:n], ps1[0:FORD, 0:n], fpt[0][:, 1:2], fpt[0][:, 4:5], ALU.mult, ALU.add,
                 [ps1d, fpt[1]], [a1d])
            sin_reduced(a1_[:, 0:n], a1d, n)
            ps2, ps2d = cx.psum()
            p.mm(ps2[0:FORD, 0:n], w2t[0], a1_[:, 0:n], True, True, [w2t[1], a1d], ps2d)
            p.ts('vector', a2_[:, 0:n], ps2[0:FORD, 0:n], fpt[0][:, 3:4], fpt[0][:, 5:6], ALU.mult, ALU.add,
                 [ps2d, fpt[1]], [a2d])
            sin_reduced(a2_[:, 0:n], a2d, n)
            ps3, ps3d = cx.psum()
            p.mm(ps3[:, 0:n], w3t[0][:, ps_ * 128:(ps_ + 1) * 128], a2_[:, 0:n], True, True, [w3t[1], a2d], ps3d)
            p.tt('vector', kb_[:, 0:n], ps3[:, 0:n], dc_[:, 0:n], ALU.mult, [ps3d, dd], [kbd])
            d0 = Dep()
            if ps_ == 0:
                p.dma('sync', krev.ap()[:, KPAD + c0:KPAD + c0 + n], kb_[:, 0:n], [kbd], [d0], ksem)
            else:
                lo = 1 if c0 == 0 else 0
                p.dma('sync', krev.ap()[:, 8319 + c0 + lo:8319 + c0 + n], kb_[:, lo:n], [kbd], [d0], ksem)
            krds.append(d0)
    xc0 = (ar.f32(T), al())
    xc2 = (ar.f32(T), al())
    mk1 = ar.mark()
    ust = [(ar.f32(T), al(), 'd_u%d' % i) for i in range(2)]
    xc1 = (ar.f32(T), al())
    xcs = [xc0, xc1, xc2]
    for s in range(3):
        u_, ud, usem = ust[s % 2]
        o_, od_ = xcs[s]
        p.dma('sync' if s % 2 == 0 else 'gpsimd', u_, uT.ap()[s], [], [ud], usem)
        hw = hpt[0]
        p.ts('vector', o_, u_, hw[:, s * 3 + 1:s * 3 + 2], hw[:, 9 + s:10 + s], ALU.mult, ALU.add, [ud, hpt[1]], [od_])
        p.stt('vector', o_[:, 1:T], u_[:, 0:T - 1], hw[:, s * 3:s * 3 + 1], o_[:, 1:T], ALU.mult, ALU.add,
              [ud, hpt[1]], [od_])
        p.stt('vector', o_[:, 0:T - 1], u_[:, 1:T], hw[:, s * 3 + 2:s * 3 + 3], o_[:, 0:T - 1], ALU.mult, ALU.add,
              [ud, hpt[1]], [od_])
    p.tt('vector', xc2[0], xc2[0], xc1[0], ALU.mult, [xc1[1]], [xc2[1]])
    ar.reset(mk1)
    rdeps = [ust[0][1], ust[1][1], xc1[1]]

    def al2():
        return alias(rdeps)

    ZT = (ar.bf16(NBLK * 128), al2())
    ZT3 = ZT[0].rearrange("p (s c) -> p s c", c=128)
    G = [(ar.bf16(8320), al2(), 'd_G%d' % i) for i in range(2)]
    YT = (ar.f32(NBLK * 128), al2())
    YT3 = YT[0].rearrange("p (t c) -> p t c", c=128)
    tmpo = [(ar.f32(512), al2()) for _ in range(2)]
    p.memset('gpsimd', ZT[0], 0.0, [ZT[1]])
    for b4 in range(17):
        ps, psd = cx.psum()
        nb = min(4, NBLK - b4 * 4)
        for j in range(nb):
            blk = b4 * 4 + j
            n = min(128, T - blk * 128)
            p.mm(ps[0:n, j * 128:(j + 1) * 128], xc2[0][:, blk * 128:blk * 128 + n], cx.ident, True, True,
                 [xc2[1], cx.cst_d], psd)
        if b4 < 16:
            p.copy(cx.evq(), ZT[0][:, b4 * 512:(b4 + 1) * 512], ps[:, 0:512], [psd], [ZT[1]])
        else:
            p.copy(cx.evq(), ZT[0][0:16, 64 * 128:65 * 128], ps[0:16, 0:128], [psd], [ZT[1]])
    psY = psYd = None
    for c in range(128):
        slot = c % 7
        if slot == 0:
            psY, psYd = cx.psum()
        ga, gad, gas = G[0]
        gb, gbd, gbs = G[1]
        p.dma('sync', ga[:, 0:8320], bass.AP(krev, c * KW, [[1, 128], [1, 8320]]), krds, [gad], gas)
        p.dma('gpsimd', gb[:, 0:8192], bass.AP(krev, c * KW + 8320, [[1, 128], [1, 8192]]), krds, [gbd], gbs)
        for dl in range(0, 65):
            j = 8192 - 128 * dl
            nn = 65 - dl
            p.mm(psY[:, slot * 65 + dl:slot * 65 + dl + nn], ga[:, j:j + 128], ZT3[:, 0:nn, c], dl == 0, False,
                 [gad, ZT[1]], psYd)
        for ad in range(1, 65):
            j = 128 * ad - 128
            nn = 65 - ad
            p.mm(psY[:, slot * 65:slot * 65 + nn], gb[:, j:j + 128], ZT3[:, ad:65, c], False, ad == 64,
                 [gbd, ZT[1]], psYd)
        if slot == 6 or c == 127:
            c0 = c - slot
            nch = slot + 1
            p.copy(cx.evq(), YT3[:, :, c0:c0 + nch].rearrange("p t c -> p c t"),
                   psY[:, 0:nch * 65].rearrange("p (c t) -> p c t", t=65), [psYd], [YT[1]])
    hw = hpt[0]
    for b4 in range(17):
        ps, psd = cx.psum()
        nb = min(4, NBLK - b4 * 4)
        for j in range(nb):
            blk = b4 * 4 + j
            p.mm(ps[:, j * 128:(j + 1) * 128], YT3[:, blk, :], cx.J, True, True, [YT[1], cx.cst_d], psd)
        t0 = b4 * 512
        n = min(512, T - t0)
        tm, tmd = tmpo[b4 % 2]
        p.stt('vector', tm[:, 0:n], xc2[0][:, t0:t0 + n], hw[:, 12:13], ps[:, 0:n], ALU.mult, ALU.add,
              [xc2[1], hpt[1], psd], [tmd])
        p.tt('gpsimd', xc0[0][:, t0:t0 + n], tm[:, 0:n], xc0[0][:, t0:t0 + n], ALU.mult, [tmd], [xc0[1]])
    od = Dep()
    p.dma('sync', y_o.ap(), xc0[0], [xc0[1]], [od], 'd_yo')
    p.out_deps.append(od)
    p.finish()
    return p.build()


def hyena_tables():
    L = T
    t = np.linspace(0.0, 1.0, L, dtype=np.float32)[:, None]
    wpos = (np.float32(2.0 * np.pi) * np.arange(L, dtype=np.float32) / np.float32(L)).astype(np.float32)
    freqs = np.linspace(1e-4, 15, 16, dtype=np.float32)
    ang = (wpos[:, None] * freqs[None, :]).astype(np.float32)
    z = np.concatenate([t, np.cos(ang), -np.sin(ang)], axis=-1).astype(np.float32)
    maxd = np.log(1e-2) / 0.3
    mind = np.log(1e-2) / 1.5
    deltas = np.abs(np.linspace(mind, maxd, HYW, dtype=np.float32))
    decay = np.exp(-t * deltas[None, :]).astype(np.float32)
    zt = np.concatenate([z[::-1].T, z.T], axis=1)
    dec = np.concatenate([decay[::-1].T, decay.T], axis=1)
    return np.ascontiguousarray(zt), np.ascontiguousarray(dec)


def build_C():
    cx = Ctx()
    p, nc, ar = cx.p, cx.nc, cx.ar
    hT = cx.din('hT', [D, TC])
    aT = cx.din('aT', [HYW, TC])
    yT = cx.din('yT', [HYW, TC])
    w_out = cx.din('w_out', [D, D])
    w_gate = cx.din('w_gate', [D, DFF])
    w_up = cx.din('w_up', [D, DFF])
    w_down = cx.din('w_down', [DFF, D])
    gv = cx.din('gv', [128, 32])
    hT_o = cx.dout('hTo', [D, TC])
    h1s = nc.dram_tensor('h1s', [D, TC], F32, kind="Internal")
    cx.load_consts()
    alloc_rms_scratch(cx)
    cx.wi = 0
    g = ar.f32(32)
    gd = Dep()
    p.dma('sync', g, gv.ap(), [], [gd], 'd_g')
    wreg = ar.bf16(16896)
    ws4 = [(wreg[:, i * 4096:(i + 1) * 4096], Dep(), 'd_w%d' % i) for i in range(4)]
    mix = [(ar.bf16(TC), Dep()) for _ in range(16)]
    hb = [(ar.f32(TC), Dep(), 'd_hb%d' % i) for i in range(2)]
    sg = [(ar.f32(342), Dep()) for _ in range(2)]
    mk = ar.mark()
    h = [(ar.f32(TC), Dep()) for _ in range(16)]
    stg = [(ar.f32(TC), Dep()) for _ in range(8)]
    for kc in range(16):
        p.dma('sync', h[kc][0], hT.ap()[kc * 128:(kc + 1) * 128, :], [], [h[kc][1]], 'd_h%d' % (kc % 4))
    for grp, src in enumerate((aT, yT)):
        for i in range(8):
            p.dma('gpsimd', stg[i][0], src.ap()[i * 128:(i + 1) * 128, :], [], [stg[i][1]], 'd_s%d' % (i % 4))
        for (t0, tn) in CH:
            r, rd = rms_rstd(cx, [(stg[i][0][:, t0:t0 + tn], [stg[i][1]], 128) for i in range(8)], HYW, tn)
            for i in range(8):
                gc = 16 + grp * 8 + i
                p.stt('vector', mix[grp * 8 + i][0][:, t0:t0 + tn], stg[i][0][:, t0:t0 + tn], g[:, gc:gc + 1],
                      r[:, 0:tn], ALU.mult, ALU.mult, [stg[i][1], gd, rd], [mix[grp * 8 + i][1]])

    def evac_out(m0, msz, ci, t0, tn, ps, psd):
        hc, hd = h[m0 // 128]
        p.tt('vector', hc[:, t0:t0 + tn], ps[0:msz, 0:tn], hc[:, t0:t0 + tn], ALU.add, [psd], [hd])

    xs = [(mix[i][0], mix[i][1], 128) for i in range(16)]
    linear(cx, xs, w_out.ap(), D, [(i * 128, 128) for i in range(16)], 256, evac_out, ws4)
    h1d = []
    for kc in range(16):
        d0 = Dep()
        p.dma('sync', h1s.ap()[kc * 128:(kc + 1) * 128, :], h[kc][0], [h[kc][1]], [d0], 'd_sp%d' % (kc % 4))
        h1d.append(d0)
    for (t0, tn) in CH:
        r, rd = rms_rstd(cx, [(h[kc][0][:, t0:t0 + tn], [h[kc][1]], 128) for kc in range(16)], D, tn)
        for kc in range(16):
            p.stt('vector', mix[kc][0][:, t0:t0 + tn], h[kc][0][:, t0:t0 + tn], g[:, kc:kc + 1], r[:, 0:tn],
                  ALU.mult, ALU.mult, [h[kc][1], gd, rd], [mix[kc][1]])
    ar.reset(mk)
    olds = [x[1] for x in h] + [x[1] for x in stg]
    act = [(ar.bf16(TC), alias(olds)) for _ in range(DFF // 128)]
    si = 0
    for blk in range(DFF // 256):
        b0 = blk * 256
        sl = []
        for wi_, wsrc in enumerate((w_gate, w_up)):
            wap, wd, wsem = ws4[(blk % 2) * 2 + wi_]
            wv = wap[:, 0:16 * 256].rearrange("p (k m) -> p k m", m=256)
            p.dma('gpsimd', wv, wsrc.ap()[:, b0:b0 + 256].rearrange("(k p) m -> p k m", p=128),
                  [], [wd], wsem)
            sl.append((wv, wd))
        for mc in range(2):
            m = blk * 2 + mc
            for (t0, tn) in CH:
                psg, psgd = cx.psum()
                for kc in range(16):
                    p.mm(psg[:, 0:tn], sl[0][0][:, kc, mc * 128:(mc + 1) * 128], mix[kc][0][:, t0:t0 + tn],
                         kc == 0, kc == 15, [sl[0][1], mix[kc][1]], psgd)
                psu, psud = cx.psum()
                for kc in range(16):
                    p.mm(psu[:, 0:tn], sl[1][0][:, kc, mc * 128:(mc + 1) * 128], mix[kc][0][:, t0:t0 + tn],
                         kc == 0, kc == 15, [sl[1][1], mix[kc][1]], psud)
                s_, sd = sg[si % 2]
                si += 1
                p.act(s_[:, 0:tn], psg[:, 0:tn], AF.Silu, [psgd], [sd])
                p.tt('vector', act[m][0][:, t0:t0 + tn], s_[:, 0:tn], psu[:, 0:tn], ALU.mult, [sd, psud], [act[m][1]])
    wold = [x[1] for x in ws4]
    ws3 = [(wreg[:, i * 5632:(i + 1) * 5632], alias(wold), 'd_w%d' % i) for i in range(3)]

    def evac_down(m0, msz, ci, t0, tn, ps, psd):
        mc = m0 // 128
        hb_, hbd, hsem = hb[mc % 2]
        if ci == 0:
            p.dma('sync', hb_, h1s.ap()[mc * 128:(mc + 1) * 128, :], [h1d[mc]], [hbd], hsem)
        p.tt('vector', hb_[:, t0:t0 + tn], ps[0:msz, 0:tn], hb_[:, t0:t0 + tn], ALU.add, [psd], [hbd])
        if ci == len(CH) - 1:
            od = Dep()
            p.dma('sync', hT_o.ap()[mc * 128:(mc + 1) * 128, :], hb_, [hbd], [od], hsem)
            p.out_deps.append(od)

    xs = [(act[i][0], act[i][1], 128) for i in range(DFF // 128)]
    linear(cx, xs, w_down.ap(), DFF, [(i * 128, 128) for i in range(16)], 128, evac_down, ws3)
    p.finish()
    return p.build()


_PROGS = {}


def _prog(name):
    if name not in _PROGS:
        _PROGS[name] = {'A': build_A, 'B': build_B, 'C': build_C}[name]()
    return _PROGS[name]


def _run(name, in_maps):
    res = run_bass_kernel_spmd(_prog(name), in_maps, core_ids=list(range(NCORE)))
    return res.results


def kernel(x, meta_tokens, norm_mix_g, w_in, q_lat_g, kv_lat_g, w_uq, w_ukv, q_norm_g, k_norm_g,
           conv_w, conv_b, filt_w1, filt_b1, filt_freq1, filt_w2, filt_b2, filt_freq2, filt_w3,
           hy_skip, attn_out_g, hy_out_g, w_out, norm_ffn_g, w_gate, w_up, w_down):
    f32 = np.float32
    ca = np.ascontiguousarray
    x = np.asarray(x, f32)
    h = np.concatenate([np.asarray(meta_tokens, f32), x[0]], axis=0)
    hT = [ca(h[c * TC:(c + 1) * TC].T) for c in range(NCORE)]
    cst = host_consts()
    cos, sin = rope_tables_np()
    cs = [ca(np.concatenate([cos[c * TC:(c + 1) * TC].T, sin[c * TC:(c + 1) * TC].T], axis=1)) for c in range(NCORE)]
    ztab, dtab = hyena_tables()
    for l in range(DEPTH):
        gA = pack_gains_A(np.asarray(norm_mix_g[l], f32), np.asarray(q_lat_g[l], f32), np.asarray(kv_lat_g[l], f32),
                          np.asarray(q_norm_g[l], f32), np.asarray(k_norm_g[l], f32))
        wi, wq, wkv = ca(np.asarray(w_in[l], f32)), ca(np.asarray(w_uq[l], f32)), ca(np.asarray(w_ukv[l], f32))
        rA = _run('A', [{'hT': hT[c], 'w_in': wi, 'w_uq': wq, 'w_ukv': wkv, 'gv': gA, 'cs': cs[c], 'cst': cst}
                        for c in range(NCORE)])
        qT = np.concatenate([r['qT'] for r in rA], axis=2)
        kT = np.concatenate([r['kT'] for r in rA], axis=2)
        v = np.concatenate([r['v'] for r in rA], axis=0)
        uT = np.concatenate([r['uT'] for r in rA], axis=1)
        cw = np.asarray(conv_w[l], f32)
        cb = np.asarray(conv_b[l], f32)
        w3 = np.asarray(filt_w3[l], f32)
        fpp = ca(np.stack([np.asarray(filt_b1[l], f32), np.asarray(filt_freq1[l], f32),
                           np.asarray(filt_b2[l], f32), np.asarray(filt_freq2[l], f32)], axis=1))
        mapsB = []
        for j in range(NCORE):
            ch = slice(j * 128, (j + 1) * 128)
            hp = np.zeros((128, 13), f32)
            for s in range(3):
                for tap in range(3):
                    hp[:, s * 3 + tap] = cw[tap, s * HYW + j * 128:s * HYW + (j + 1) * 128]
                hp[:, 9 + s] = cb[s * HYW + j * 128:s * HYW + (j + 1) * 128]
            hp[:, 12] = np.asarray(hy_skip[l], f32)[ch]
            mapsB.append({'qT': ca(qT[j]), 'kT': ca(kT[j]), 'v': ca(v[:, ch]),
                          'uT': ca(np.stack([uT[s * HYW + j * 128:s * HYW + (j + 1) * 128] for s in range(3)])),
                          'hp': hp, 'fw1': ca(np.asarray(filt_w1[l], f32)), 'fw2': ca(np.asarray(filt_w2[l], f32)),
                          'fw3': ca(np.concatenate([w3[:, ch], w3[:, HYW + j * 128:HYW + (j + 1) * 128]], axis=1)),
                          'fpp': fpp, 'zt': ztab, 'dec': ca(dtab[ch]), 'cst': cst})
        rB = _run('B', mapsB)
        a = np.concatenate([r['a'] for r in rB], axis=1)
        yT = np.concatenate([r['yT'] for r in rB], axis=0)
        gC = np.zeros((128, 32), f32)
        gC[:, 0:16] = np.asarray(norm_ffn_g[l], f32).reshape(16, 128).T
        gC[:, 16:24] = np.asarray(attn_out_g[l], f32).reshape(8, 128).T
        gC[:, 24:32] = np.asarray(hy_out_g[l], f32).reshape(8, 128).T
        wo, wg, wu, wd = (ca(np.asarray(w_out[l], f32)), ca(np.asarray(w_gate[l], f32)),
                          ca(np.asarray(w_up[l], f32)), ca(np.asarray(w_down[l], f32)))
        rC = _run('C', [{'hT': hT[c], 'aT': ca(a[c * TC:(c + 1) * TC].T), 'yT': ca(yT[:, c * TC:(c + 1) * TC]),
                         'w_out': wo, 'w_gate': wg, 'w_up': wu, 'w_down': wd, 'gv': gC, 'cst': cst}
                        for c in range(NCORE)])
        hT = [r['hTo'] for r in rC]
    hfull = np.concatenate([t.T for t in hT], axis=0)
    return ca(hfull[NMETA:][None].astype(f32))
```

```python
import numpy as np
import ml_dtypes
from contextlib import ExitStack
import concourse.bass as bass
import concourse.mybir as mybir
from concourse.bass_utils import run_bass_kernel_spmd

F32 = mybir.dt.float32
BF16 = mybir.dt.bfloat16
AF = mybir.ActivationFunctionType
ALU = mybir.AluOpType
ENGS = ('sync', 'scalar', 'vector', 'gpsimd', 'tensor')

NCORE = 8
NDEV = None
D = 2048
SEQ = 8192
NMETA = 16
T = SEQ + NMETA
TC = T // NCORE
CH = [(0, 342), (342, 342), (684, 342)]
DEPTH = 4
NH = 8
QLORA = 512
KVLORA = 256
ROPE = 64
NOPE = 128
QK = 192
VD = 128
HYW = 1024
DFF = 5632
INCOLS = 3904
EPS = 1e-6
FEMB = 33
FORD = 64
NBLK = 65
KW = 16639
KPAD = 112


class Dep:
    __slots__ = ('w', 'r')

    def __init__(s):
        s.w = None
        s.r = {}


class SP:
    __slots__ = ('sem', 'idx', 'know')

    def __init__(s, sem, idx, know):
        s.sem = sem
        s.idx = idx
        s.know = know


def alias(olds):
    d = Dep()
    for o in olds:
        sps = list(o.r.values())
        if o.w is not None:
            sps.append(o.w)
        for sp in sps:
            if sp.sem not in d.r or d.r[sp.sem].idx < sp.idx:
                d.r[sp.sem] = sp
    return d


class Prog:
    def __init__(self):
        self.nc = bass.Bass("TRN2", target_bir_lowering=False, num_devices=NDEV)
        self.ops = {e: [] for e in ENGS}
        self.know = {e: {} for e in ENGS}
        self.cnt = {}
        self.targets = {}
        self.nps = 0
        self.out_deps = []

    def emit(self, eng, fn, reads=(), writes=(), dsem=None, strict=False):
        is_dma = dsem is not None
        own = 'E_' + eng
        sem = dsem if is_dma else own
        know = self.know[eng]
        deps = []
        for d in reads:
            if d.w is not None:
                deps.append(d.w)
        for d in writes:
            if d.w is not None:
                deps.append(d.w)
            deps.extend(d.r.values())
        waits = {}
        for sp in deps:
            if (not is_dma) and sp.sem == own and not strict:
                continue
            if know.get(sp.sem, 0) >= sp.idx:
                continue
            waits[sp.sem] = max(waits.get(sp.sem, 0), sp.idx)
            for s, v in sp.know.items():
                if know.get(s, 0) < v:
                    know[s] = v
        for s, v in waits.items():
            self.targets.setdefault(s, set()).add(v)
        if fn is None:
            self.ops[eng].append((None, waits, None, 0))
            return
        idx = self.cnt.get(sem, 0) + 1
        self.cnt[sem] = idx
        snap = dict(know)
        snap[sem] = idx
        sp = SP(sem, idx, snap)
        for d in reads:
            if sem not in d.r or d.r[sem].idx < idx:
                d.r[sem] = sp
        for d in writes:
            d.w = sp
            d.r = {}
        self.ops[eng].append((fn, waits, sem, idx))

    def finish(self):
        self.emit('sync', None, reads=self.out_deps)

    def build(self):
        nc = self.nc
        rank = {}
        for s in self.cnt:
            if s.startswith('E_'):
                t = sorted(self.targets.get(s, ()))
                rank[s] = {v: i + 1 for i, v in enumerate(t)}
        with ExitStack() as st:
            sems = {s: st.enter_context(nc.semaphore(s)) for s in self.cnt}
            block = st.enter_context(nc.Block())

            def run(e, eng):
                for fn, waits, sem, idx in self.ops[eng]:
                    for s, v in waits.items():
                        val = rank[s][v] if s.startswith('E_') else 16 * v
                        e.wait_ge(sems[s], val)
                    if fn is None:
                        continue
                    if sem == 'RAW':
                        fn(e)
                        continue
                    ins = fn(e)
                    if sem.startswith('E_'):
                        if idx in rank[sem]:
                            ins.then_inc(sems[sem], 1)
                    else:
                        ins.then_inc(sems[sem], 16)

            @block.sync
            def _(e):
                run(e, 'sync')

            @block.scalar
            def _(e):
                run(e, 'scalar')

            @block.vector
            def _(e):
                run(e, 'vector')

            @block.gpsimd
            def _(e):
                run(e, 'gpsimd')

            @block.tensor
            def _(e):
                run(e, 'tensor')
        return nc

    def mm(self, ps, lhsT, rhs, start, stop, rd, psd):
        self.emit('tensor', lambda e: e.matmul(ps, lhsT=lhsT, rhs=rhs, start=start, stop=stop),
                  reads=rd, writes=[psd])

    def act(self, out, in_, func, rd, wr, scale=1.0, bias=None):
        if bias is None:
            self.emit('scalar', lambda e: e.activation(out=out, in_=in_, func=func, scale=scale),
                      reads=rd, writes=wr)
        else:
            self.emit('scalar', lambda e: e.activation(out=out, in_=in_, func=func, scale=scale, bias=bias),
                      reads=rd, writes=wr)

    def copy(self, eng, out, in_, rd, wr):
        if eng == 'scalar':
            self.emit('scalar', lambda e: e.copy(out=out, in_=in_), reads=rd, writes=wr)
        else:
            self.emit(eng, lambda e: e.tensor_copy(out=out, in_=in_), reads=rd, writes=wr)

    def tt(self, eng, out, in0, in1, op, rd, wr):
        self.emit(eng, lambda e: e.tensor_tensor(out=out, in0=in0, in1=in1, op=op), reads=rd, writes=wr)

    def ts(self, eng, out, in0, s1, s2, op0, op1, rd, wr, strict=False):
        if s2 is None:
            self.emit(eng, lambda e: e.tensor_scalar(out=out, in0=in0, scalar1=s1, scalar2=None, op0=op0),
                      reads=rd, writes=wr, strict=strict)
        else:
            self.emit(eng, lambda e: e.tensor_scalar(out=out, in0=in0, scalar1=s1, scalar2=s2, op0=op0, op1=op1),
                      reads=rd, writes=wr, strict=strict)

    def stt(self, eng, out, in0, scalar, in1, op0, op1, rd, wr):
        self.emit(eng, lambda e: e.scalar_tensor_tensor(out=out, in0=in0, scalar=scalar, in1=in1, op0=op0, op1=op1),
                  reads=rd, writes=wr)

    def recip(self, out, in_, rd, wr):
        self.emit('vector', lambda e: e.reciprocal(out=out, in_=in_), reads=rd, writes=wr)

    def memset(self, eng, ap, val, wr):
        self.emit(eng, lambda e: e.memset(ap, val), writes=wr)

    def dma(self, eng, out, in_, rd, wr, sem, is_out=False):
        self.emit(eng, lambda e: e.dma_start(out=out, in_=in_), reads=rd, writes=wr, dsem=sem)


class Arena:
    def __init__(s, nc, cols=47600):
        s.t = nc.alloc_sbuf_tensor("arena", [128, cols], F32)
        s.off = 0
        s.cols = cols

    def f32(s, n, rows=128):
        a = s.t[0:rows, s.off:s.off + n]
        s.off += n
        assert s.off <= s.cols, (s.off, s.cols)
        return a

    def bf16(s, n, rows=128):
        m = (n + 1) // 2
        a = s.t[0:rows, s.off:s.off + m].bitcast(BF16)
        s.off += m
        assert s.off <= s.cols, (s.off, s.cols)
        return a

    def mark(s):
        return s.off

    def reset(s, m):
        s.off = m


class Ctx:
    def __init__(self):
        self.p = Prog()
        self.nc = self.p.nc
        self.ar = Arena(self.nc)
        self.banks = []
        for i in range(8):
            t = self.nc.alloc_psum_tensor("psb%d" % i, [128, 512], F32)
            self.banks.append((t, Dep()))
        self.bi = 0
        self.dq = 0
        self.ev = 0

    def psum(self):
        b = self.banks[self.bi % 8]
        self.bi += 1
        return b

    def din(self, name, shape, dt=F32):
        return self.nc.dram_tensor(name, list(shape), dt, kind="ExternalInput")

    def dout(self, name, shape, dt=F32):
        return self.nc.dram_tensor(name, list(shape), dt, kind="ExternalOutput")

    def load_consts(self):
        c = self.din('cst', [128, 128 * 3 + 64])
        self.cst = self.ar.f32(128 * 3 + 64)
        self.cst_d = Dep()
        self.p.dma('sync', self.cst, c.ap(), [], [self.cst_d], 'd_cst')
        self.ones = self.cst[:, 0:128]
        self.ident = self.cst[:, 128:256]
        self.J = self.cst[:, 256:384]
        self.R = self.cst[0:64, 384:448]
        self.epsc = self.ar.f32(1)
        self.p.memset('vector', self.epsc, EPS, [self.cst_d])

    def evq(self):
        self.ev += 1
        return 'scalar' if self.ev % 2 else 'vector'


def host_consts():
    c = np.zeros((128, 448), np.float32)
    c[:, 0:128] = 1.0
    c[:, 128:256] = np.eye(128, dtype=np.float32)
    c[:, 256:384] = np.eye(128, dtype=np.float32)[::-1]
    Rm = np.zeros((64, 64), np.float32)
    for m in range(32):
        Rm[m + 32, m] = -1.0
        Rm[m, m + 32] = 1.0
    c[0:64, 384:448] = Rm
    return c


def rms_rstd(cx, srcs, nfeat, width):
    p = cx.p
    ps, psd = cx.psum()
    n = len(srcs)
    for i, (ap, deps, rows) in enumerate(srcs):
        sq = cx.sq[cx.sqi % 2]
        sqd = cx.sqd[cx.sqi % 2]
        cx.sqi += 1
        p.act(sq[0:rows, 0:width], ap, AF.Square, deps, [sqd])
        p.mm(ps[:, 0:width], cx.ones[0:rows, :], sq[0:rows, 0:width], i == 0, i == n - 1, [sqd, cx.cst_d], psd)
    r = cx.rs[cx.rsi % 2]
    rd = cx.rsd[cx.rsi % 2]
    cx.rsi += 1
    p.act(r[:, 0:width], ps[:, 0:width], AF.Sqrt, [psd, cx.cst_d], [rd], scale=1.0 / nfeat, bias=cx.epsc[:, 0:1])
    p.recip(r[:, 0:width], r[:, 0:width], [rd], [rd])
    return r, rd


def alloc_rms_scratch(cx):
    cx.sq = [cx.ar.f32(342), cx.ar.f32(342)]
    cx.sqd = [Dep(), Dep()]
    cx.sqi = 0
    cx.rs = [cx.ar.f32(342), cx.ar.f32(342)]
    cx.rsd = [Dep(), Dep()]
    cx.rsi = 0


def linear(cx, xs, w, K, m_chunks, mblk, evac, wslots):
    p = cx.p
    KC = len(xs)
    blocks = []
    cur = []
    cur0 = None
    for (m0, msz) in m_chunks:
        if cur and (m0 + msz - cur0 > mblk or m0 != cur[-1][0] + cur[-1][1]):
            blocks.append((cur0, cur))
            cur = []
        if not cur:
            cur0 = m0
        cur.append((m0, msz))
    if cur:
        blocks.append((cur0, cur))
    for (b0, chunks) in blocks:
        bw = chunks[-1][0] + chunks[-1][1] - b0
        slot = wslots[cx.wi % len(wslots)]
        cx.wi += 1
        wap, wd, wsem = slot
        wv = wap[:, 0:KC * mblk].rearrange("p (k m) -> p k m", m=mblk)
        rows_last = xs[-1][2]
        if rows_last == 128:
            src = w[0:KC * 128, b0:b0 + bw].rearrange("(k p) m -> p k m", p=128)
            p.dma('gpsimd', wv[:, :, 0:bw], src, [], [wd], wsem)
        else:
            if KC > 1:
                src = w[0:(KC - 1) * 128, b0:b0 + bw].rearrange("(k p) m -> p k m", p=128)
                p.dma('gpsimd', wv[:, 0:KC - 1, 0:bw], src, [], [wd], wsem)
            src = w[(KC - 1) * 128:(KC - 1) * 128 + rows_last, b0:b0 + bw]
            p.dma('gpsimd', wv[0:rows_last, KC - 1, 0:bw], src, [], [wd], wsem)
        for (m0, msz) in chunks:
            for ci, (t0, tn) in enumerate(CH):
                ps, psd = cx.psum()
                for kc, (xap, xd, rows) in enumerate(xs):
                    p.mm(ps[0:msz, 0:tn], wv[0:rows, kc, m0 - b0:m0 - b0 + msz], xap[0:rows, t0:t0 + tn],
                         kc == 0, kc == KC - 1, [wd, xd], psd)
                evac(m0, msz, ci, t0, tn, ps, psd)


def build_A():
    cx = Ctx()
    p, nc, ar = cx.p, cx.nc, cx.ar
    hT = cx.din('hT', [D, TC])
    w_in = cx.din('w_in', [D, INCOLS])
    w_uq = cx.din('w_uq', [QLORA, NH * QK])
    w_ukv = cx.din('w_ukv', [KVLORA, NH * (NOPE + VD)])
    gv = cx.din('gv', [128, 26])
    cs = cx.din('cs', [64, 2 * TC])
    qT_o = cx.dout('qT', [NH, QK, TC], BF16)
    kT_o = cx.dout('kT', [NH, QK, TC], BF16)
    v_o = cx.dout('v', [TC, NH * VD], BF16)
    uT_o = cx.dout('uT', [3 * HYW, TC])
    cx.load_consts()
    alloc_rms_scratch(cx)
    cx.wi = 0
    g = ar.f32(26)
    gd = Dep()
    p.dma('sync', g, gv.ap(), [], [gd], 'd_g')
    cst = ar.f32(2 * TC, rows=64)
    csd = Dep()
    p.dma('sync', cst, cs.ap(), [], [csd], 'd_cs')
    cos = cst[:, 0:TC]
    sin = cst[:, TC:2 * TC]
    wslots = [(ar.bf16(16 * 512), Dep(), 'd_w%d' % i) for i in range(2)]
    cq = [(ar.f32(TC), Dep()) for _ in range(4)]
    ckv = [(ar.f32(TC), Dep()) for _ in range(2)]
    kr = (ar.f32(TC, rows=64), Dep())
    stage = [(ar.f32(TC), Dep(), 'd_st%d' % i) for i in range(2)]
    hn = [(ar.bf16(TC), Dep()) for _ in range(16)]
    mk = ar.mark()
    h = [(ar.f32(TC), Dep()) for _ in range(16)]
    for kc in range(16):
        p.dma('sync', h[kc][0], hT.ap()[kc * 128:(kc + 1) * 128, :], [], [h[kc][1]], 'd_h%d' % kc)
    for (t0, tn) in CH:
        r, rd = rms_rstd(cx, [(h[kc][0][:, t0:t0 + tn], [h[kc][1]], 128) for kc in range(16)], D, tn)
        for kc in range(16):
            p.stt('vector', hn[kc][0][:, t0:t0 + tn], h[kc][0][:, t0:t0 + tn], g[:, kc:kc + 1],
                  r[:, 0:tn], ALU.mult, ALU.mult, [h[kc][1], gd, rd], [hn[kc][1]])
    sti = [0]

    def evac_in(m0, msz, ci, t0, tn, ps, psd):
        if m0 < 512:
            dst, dd = cq[m0 // 128]
            p.copy(cx.evq(), dst[:, t0:t0 + tn], ps[0:msz, 0:tn], [psd], [dd])
        elif m0 < 768:
            dst, dd = ckv[(m0 - 512) // 128]
            p.copy(cx.evq(), dst[:, t0:t0 + tn], ps[0:msz, 0:tn], [psd], [dd])
        elif m0 < 832:
            p.copy(cx.evq(), kr[0][:, t0:t0 + tn], ps[0:64, 0:tn], [psd], [kr[1]])
        else:
            sap, sd, ssem = stage[sti[0] % 2]
            p.copy(cx.evq(), sap[:, t0:t0 + tn], ps[0:msz, 0:tn], [psd], [sd])
            if ci == len(CH) - 1:
                od = Dep()
                p.dma('sync', uT_o.ap()[m0 - 832:m0 - 832 + 128, :], sap, [sd], [od], ssem)
                p.out_deps.append(od)
                sti[0] += 1

    m_chunks = [(i * 128, 128) for i in range(6)] + [(768, 64)] + [(832 + i * 128, 128) for i in range(24)]
    xs = [(hn[kc][0], hn[kc][1], 128) for kc in range(16)]
    linear(cx, xs, w_in.ap(), D, m_chunks, 512, evac_in, wslots)
    ar.reset(mk)
    hdeps = [x[1] for x in h]
    ad = alias(hdeps)

    def al():
        return alias(hdeps)

    cqn = [(ar.bf16(TC), al()) for _ in range(4)]
    ckvn = [(ar.bf16(TC), al()) for _ in range(2)]
    for (t0, tn) in CH:
        r, rd = rms_rstd(cx, [(cq[i][0][:, t0:t0 + tn], [cq[i][1]], 128) for i in range(4)], QLORA, tn)
        for i in range(4):
            p.stt('vector', cqn[i][0][:, t0:t0 + tn], cq[i][0][:, t0:t0 + tn], g[:, 16 + i:17 + i], r[:, 0:tn],
                  ALU.mult, ALU.mult, [cq[i][1], gd, rd], [cqn[i][1]])
        r, rd = rms_rstd(cx, [(ckv[i][0][:, t0:t0 + tn], [ckv[i][1]], 128) for i in range(2)], KVLORA, tn)
        for i in range(2):
            p.stt('vector', ckvn[i][0][:, t0:t0 + tn], ckv[i][0][:, t0:t0 + tn], g[:, 20 + i:21 + i], r[:, 0:tn],
                  ALU.mult, ALU.mult, [ckv[i][1], gd, rd], [ckvn[i][1]])
    hold = [(ar.f32(342), al()) for _ in range(2)]
    holdr = [(ar.f32(342, rows=64), al()) for _ in range(2)]
    rot = [(ar.f32(342, rows=64), al()) for _ in range(2)]
    tmpc = [(ar.f32(342, rows=64), al()) for _ in range(2)]
    outn = [(ar.bf16(TC), al(), 'd_on%d' % i) for i in range(2)]
    outr = [(ar.bf16(TC, rows=64), al(), 'd_or%d' % i) for i in range(2)]
    krsq = (ar.f32(TC, rows=64), al())
    state = {'i': 0, 'o': 0}

    def qk_head(which, hd, dst, gcol):
        pass

    wq_slots = wslots
    def head_pass(which):
        KC = 4 if which == 'q' else 2
        xs_ = cqn if which == 'q' else ckvn
        wdr = w_uq.ap() if which == 'q' else w_ukv.ap()
        gn = 22 if which == 'q' else 24
        out_dram = qT_o if which == 'q' else kT_o
        for hh in range(NH):
            wap, wd, wsem = wslots[cx.wi % 2]
            cx.wi += 1
            ncol = 192 if which == 'q' else 128
            c0 = hh * 192 if which == 'q' else hh * 256
            wv = wap[:, 0:KC * 192].rearrange("p (k m) -> p k m", m=192)
            p.dma('gpsimd', wv[:, :, 0:ncol], wdr[:, c0:c0 + ncol].rearrange("(k p) m -> p k m", p=128),
                  [], [wd], wsem)
            on, ond, onsem = outn[state['o'] % 2]
            orr, ord_, orsem = outr[state['o'] % 2]
            state['o'] += 1
            for (t0, tn) in CH:
                i = state['i'] % 2
                state['i'] += 1
                psn, psnd = cx.psum()
                for kc in range(KC):
                    p.mm(psn[:, 0:tn], wv[:, kc, 0:128], xs_[kc][0][:, t0:t0 + tn], kc == 0, kc == KC - 1,
                         [wd, xs_[kc][1]], psnd)
                srcs = [(psn[:, 0:tn], [psnd], 128)]
                if which == 'q':
                    psr, psrd = cx.psum()
                    for kc in range(KC):
                        p.mm(psr[0:64, 0:tn], wv[:, kc, 128:192], xs_[kc][0][:, t0:t0 + tn], kc == 0, kc == KC - 1,
                             [wd, xs_[kc][1]], psrd)
                    rsrc, rsd = psr[0:64, 0:tn], psrd
                else:
                    rsrc, rsd = kr[0][:, t0:t0 + tn], kr[1]
                srcs.append((rsrc, [rsd], 64))
                r, rd = rms_rstd(cx, srcs, QK, tn)
                p.stt('vector', on[:, t0:t0 + tn], psn[:, 0:tn], g[:, gn:gn + 1], r[:, 0:tn], ALU.mult, ALU.mult,
                      [psnd, gd, rd], [ond])
                hr, hrd = holdr[i]
                p.stt('vector', hr[:, 0:tn], rsrc, g[0:64, gn + 1:gn + 2], r[0:64, 0:tn], ALU.mult, ALU.mult,
                      [rsd, gd, rd], [hrd])
                pr, prd = cx.psum()
                p.mm(pr[0:64, 0:tn], cx.R, hr[:, 0:tn], True, True, [hrd, cx.cst_d], prd)
                tc_, tcd = tmpc[i]
                p.tt('vector', tc_[:, 0:tn], pr[0:64, 0:tn], sin[:, t0:t0 + tn], ALU.mult, [prd, csd], [tcd])
                p.tt('gpsimd', hr[:, 0:tn], hr[:, 0:tn], cos[:, t0:t0 + tn], ALU.mult, [hrd, csd], [hrd])
                p.tt('vector', orr[:, t0:t0 + tn], hr[:, 0:tn], tc_[:, 0:tn], ALU.add, [hrd, tcd], [ord_])
            od = Dep()
            p.dma('sync', out_dram.ap()[hh, 0:128, :], on, [ond], [od], onsem)
            p.out_deps.append(od)
            od = Dep()
            p.dma('sync', out_dram.ap()[hh, 128:192, :], orr, [ord_], [od], orsem)
            p.out_deps.append(od)

    head_pass('q')
    head_pass('k')
    wvv = [(ar.bf16(2 * 512), al(), 'd_wv%d' % i) for i in range(2)]
    vst = [(ar.bf16(512), al(), 'd_vs%d' % i) for i in range(2)]
    vi = 0
    for half in range(2):
        wap, wd, wsem = wvv[half]
        wv = wap.rearrange("p (k h m) -> p k h m", k=2, h=4)
        for hh in range(4):
            c0 = (half * 4 + hh) * 256 + 128
            p.dma('gpsimd', wv[:, :, hh, :], w_ukv.ap()[:, c0:c0 + 128].rearrange("(k p) m -> p k m", p=128),
                  [], [wd], wsem)
        wflat = wap.rearrange("p (k n) -> p k n", k=2)
        for tb in range(9):
            t0 = tb * 128
            tn = min(128, TC - t0)
            ps, psd = cx.psum()
            for kc in range(2):
                p.mm(ps[0:tn, 0:512], ckvn[kc][0][:, t0:t0 + tn], wflat[:, kc, :], kc == 0, kc == 1,
                     [wd, ckvn[kc][1]], psd)
            sap, sd, ssem = vst[vi % 2]
            vi += 1
            p.copy(cx.evq(), sap[0:tn, :], ps[0:tn, 0:512], [psd], [sd])
            od = Dep()
            p.dma('sync', v_o.ap()[t0:t0 + tn, half * 512:(half + 1) * 512], sap[0:tn, :], [sd], [od], ssem)
            p.out_deps.append(od)
    p.finish()
    return p.build()


def pack_gains_A(norm_g, q_lat_g, kv_lat_g, q_norm_g, k_norm_g):
    g = np.zeros((128, 26), np.float32)
    g[:, 0:16] = norm_g.reshape(16, 128).T
    g[:, 16:20] = q_lat_g.reshape(4, 128).T
    g[:, 20:22] = kv_lat_g.reshape(2, 128).T
    g[:, 22] = q_norm_g[0:128]
    g[0:64, 23] = q_norm_g[128:192]
    g[:, 24] = k_norm_g[0:128]
    g[0:64, 25] = k_norm_g[128:192]
    return g


def rope_tables_np():
    pos = np.arange(T, dtype=np.float32)
    inv = (np.float32(10000.0) ** (-np.arange(0, ROPE, 2, dtype=np.float32) / np.float32(ROPE))).astype(np.float32)
    ang = (pos[:, None] * inv[None, :]).astype(np.float32)
    ang = np.concatenate([ang, ang], axis=-1)
    return np.cos(ang).astype(np.float32), np.sin(ang).astype(np.float32)


def build_B():
    cx = Ctx()
    p, nc, ar = cx.p, cx.nc, cx.ar
    qT = cx.din('qT', [QK, T], BF16)
    kT = cx.din('kT', [QK, T], BF16)
    vin = cx.din('v', [T, VD], BF16)
    uT = cx.din('uT', [3, 128, T])
    hp = cx.din('hp', [128, 13])
    fw1 = cx.din('fw1', [FEMB, FORD])
    fw2 = cx.din('fw2', [FORD, FORD])
    fw3 = cx.din('fw3', [FORD, 256])
    fpp = cx.din('fpp', [FORD, 4])
    zt = cx.din('zt', [FEMB, 2 * T])
    dec = cx.din('dec', [128, 2 * T])
    a_o = cx.dout('a', [T, VD])
    y_o = cx.dout('yT', [128, T])
    krev = nc.dram_tensor('krev', [128, KW], BF16, kind="Internal")
    cx.load_consts()
    def al():
        return Dep()

    hpt = (ar.f32(13), al())
    p.dma('sync', hpt[0], hp.ap(), [], [hpt[1]], 'd_hp')
    w1t = (ar.f32(FORD, rows=FEMB), al())
    w2t = (ar.f32(FORD, rows=FORD), al())
    w3t = (ar.f32(256, rows=FORD), al())
    fpt = (ar.f32(8, rows=FORD), al())
    p.dma('sync', w1t[0], fw1.ap(), [], [w1t[1]], 'd_f1')
    p.dma('sync', w2t[0], fw2.ap(), [], [w2t[1]], 'd_f2')
    p.dma('sync', w3t[0], fw3.ap(), [], [w3t[1]], 'd_f3')
    p.dma('sync', fpt[0][:, 0:4], fpp.ap(), [], [fpt[1]], 'd_f4')
    p.tt('vector', fpt[0][:, 4:5], fpt[0][:, 0:1], fpt[0][:, 1:2], ALU.mult, [fpt[1]], [fpt[1]])
    p.tt('vector', fpt[0][:, 5:6], fpt[0][:, 2:3], fpt[0][:, 3:4], ALU.mult, [fpt[1]], [fpt[1]])
    zc = [(ar.f32(512, rows=FEMB), al(), 'd_zc%d' % i) for i in range(2)]
    dc = [(ar.f32(512), al(), 'd_dc%d' % i) for i in range(2)]
    a1 = [(ar.f32(512, rows=FORD), al()) for _ in range(2)]
    a2 = [(ar.f32(512, rows=FORD), al()) for _ in range(2)]
    kb = [(ar.bf16(512), al(), 'd_kb%d' % i) for i in range(2)]
    zz = (ar.bf16(KPAD), al())
    p.memset('vector', zz[0], 0.0, [zz[1]])
    krds = []
    d0 = Dep()
    p.dma('sync', krev.ap()[:, 0:KPAD], zz[0], [zz[1]], [d0], 'd_kz')
    krds.append(d0)
    d0 = Dep()
    p.dma('sync', krev.ap()[:, KW - KPAD:KW], zz[0], [zz[1]], [d0], 'd_kz')
    krds.append(d0)
    TWO_PI = float(2 * np.pi)
    PI_S = 3.1415925
    rr_i = (ar.f32(512, rows=FORD).bitcast(mybir.dt.int32), al())
    rr_f = (ar.f32(512, rows=FORD), al())
    sbk = cx.banks[0:4]
    obk = cx.banks[4:8]
    sic = [0]

    def sbank():
        b = sbk[sic[0] % 4]
        sic[0] += 1
        return b

    def sin_reduced(a, ad, n):
        p.ts('vector', rr_i[0][:, 0:n], a, 1.0 / TWO_PI, None, ALU.mult, None, [ad], [rr_i[1]])
        p.copy('vector', rr_f[0][:, 0:n], rr_i[0][:, 0:n], [rr_i[1]], [rr_f[1]])
        p.stt('vector', a, rr_f[0][:, 0:n], -TWO_PI, a, ALU.mult, ALU.add, [rr_f[1], ad], [ad])
        p.ts('vector', a, a, -PI_S, PI_S, ALU.max, ALU.min, [ad], [ad])
        p.act(a, a, AF.Sin, [ad], [ad])

    def filter_stages():
        it = 0
        for ps_ in range(2):
            for c in range(17):
                c0 = c * 512
                n = min(512, T - c0)
                zt_, zd, zsem = zc[it % 2]
                dc_, dd, dsem = dc[it % 2]
                a1_, a1d = a1[it % 2]
                a2_, a2d = a2[it % 2]
                kb_, kbd, ksem = kb[it % 2]
                it += 1
                p.dma('sync', zt_[:, 0:n], zt.ap()[:, ps_ * T + c0:ps_ * T + c0 + n], [], [zd], zsem)
                p.dma('sync', dc_[:, 0:n], dec.ap()[:, ps_ * T + c0:ps_ * T + c0 + n], [], [dd], dsem)
                yield
                ps1, ps1d = sbank()
                p.mm(ps1[0:FORD, 0:n], w1t[0], zt_[:, 0:n], True, True, [w1t[1], zd], ps1d)
                p.ts('vector', a1_[:, 0:n], ps1[0:FORD, 0:n], fpt[0][:, 1:2], fpt[0][:, 4:5], ALU.mult, ALU.add,
                     [ps1d, fpt[1]], [a1d])
                sin_reduced(a1_[:, 0:n], a1d, n)
                yield
                ps2, ps2d = sbank()
                p.mm(ps2[0:FORD, 0:n], w2t[0], a1_[:, 0:n], True, True, [w2t[1], a1d], ps2d)
                p.ts('vector', a2_[:, 0:n], ps2[0:FORD, 0:n], fpt[0][:, 3:4], fpt[0][:, 5:6], ALU.mult, ALU.add,
                     [ps2d, fpt[1]], [a2d])
                sin_reduced(a2_[:, 0:n], a2d, n)
                yield
                ps3, ps3d = sbank()
                p.mm(ps3[:, 0:n], w3t[0][:, ps_ * 128:(ps_ + 1) * 128], a2_[:, 0:n], True, True, [w3t[1], a2d], ps3d)
                p.tt('vector', kb_[:, 0:n], ps3[:, 0:n], dc_[:, 0:n], ALU.mult, [ps3d, dd], [kbd])
                d0 = Dep()
                if ps_ == 0:
                    p.dma('sync', krev.ap()[:, KPAD + c0:KPAD + c0 + n], kb_[:, 0:n], [kbd], [d0], ksem)
                else:
                    lo = 1 if c0 == 0 else 0
                    p.dma('sync', krev.ap()[:, 8319 + c0 + lo:8319 + c0 + n], kb_[:, lo:n], [kbd], [d0], ksem)
                krds.append(d0)
                yield

    fgen = filter_stages()
    mk0 = ar.mark()
    qn = (ar.bf16(T), Dep())
    qr = (ar.bf16(T, rows=64), Dep())
    kn = (ar.bf16(T), Dep())
    kr = (ar.bf16(T, rows=64), Dep())
    p.dma('sync', qn[0], qT.ap()[0:128, :], [], [qn[1]], 'd_q0')
    p.dma('sync', qr[0], qT.ap()[128:192, :], [], [qr[1]], 'd_q1')
    p.dma('gpsimd', kn[0], kT.ap()[0:128, :], [], [kn[1]], 'd_k0')
    p.dma('gpsimd', kr[0], kT.ap()[128:192, :], [], [kr[1]], 'd_k1')
    Vt = ar.bf16(NBLK * 130)
    Vd = Dep()
    V3 = Vt[:, 0:NBLK * 130].rearrange("p (k d) -> p k d", d=130)
    p.memset('vector', Vt, 1.0, [Vd])
    for k8 in range(8):
        p.dma('sync', V3[:, k8 * 8:(k8 + 1) * 8, 0:128],
              vin.ap()[k8 * 1024:(k8 + 1) * 1024, :].rearrange("(k p) d -> p k d", p=128), [], [Vd], 'd_v')
    p.dma('sync', V3[0:16, 64, 0:128], vin.ap()[8192:T, :], [], [Vd], 'd_v')
    pts = [(ar.bf16(512), Dep()) for _ in range(4)]
    rcs = [(ar.f32(4), Dep()) for _ in range(2)]
    osb = [(ar.f32(512), Dep(), 'd_os%d' % i) for i in range(2)]
    scale = float(QK) ** -0.5
    att_deps = [qn[1], qr[1], kn[1], kr[1], Vd] + [x[1] for x in pts] + [x[1] for x in rcs] + [x[1] for x in osb]
    iters = [(qc, kt) for qc in range(17) for kt in range(NBLK)]
    ptof = {}

    def emit_S(i):
        qc, kt = iters[i]
        q0 = qc * 512
        qsz = min(512, T - q0)
        k0 = kt * 128
        ksz = min(128, T - k0)
        ps, psd = sbank()
        p.mm(ps[0:ksz, 0:qsz], kn[0][:, k0:k0 + ksz], qn[0][:, q0:q0 + qsz], True, False, [kn[1], qn[1]], psd)
        p.mm(ps[0:ksz, 0:qsz], kr[0][:, k0:k0 + ksz], qr[0][:, q0:q0 + qsz], False, True, [kr[1], qr[1]], psd)
        pt, ptd = pts[i % 4]
        p.act(pt[0:ksz, 0:qsz], ps[0:ksz, 0:qsz], AF.Exp, [psd], [ptd], scale=scale)
        ptof[i] = (pt, ptd)

    def emit_PV(i):
        qc, kt = iters[i]
        q0 = qc * 512
        qsz = min(512, T - q0)
        ksz = min(128, T - kt * 128)
        nsub = (qsz + 127) // 128
        pt, ptd = ptof.pop(i)
        for sub in range(nsub):
            s0 = sub * 128
            ssz = min(128, qsz - s0)
            p.mm(obk[sub][0][0:ssz, 0:129], pt[0:ksz, s0:s0 + ssz], V3[0:ksz, kt, 0:129], kt == 0, kt == NBLK - 1,
                 [ptd, Vd], obk[sub][1])
        if kt != NBLK - 1:
            return
        rc, rcd = rcs[qc % 2]
        ob_, obd, osem = osb[qc % 2]
        ob3 = ob_.rearrange("p (s d) -> p s d", d=128)
        for sub in range(nsub):
            s0 = sub * 128
            ssz = min(128, qsz - s0)
            p.recip(rc[0:ssz, sub:sub + 1], obk[sub][0][0:ssz, 128:129], [obk[sub][1]], [rcd])
        for sub in range(nsub):
            s0 = sub * 128
            ssz = min(128, qsz - s0)
            p.ts('vector', ob3[0:ssz, sub, :], obk[sub][0][0:ssz, 0:128], rc[0:ssz, sub:sub + 1], None, ALU.mult, None,
                 [obk[sub][1], rcd], [obd], strict=True)
        od = Dep()
        if qsz == 512:
            p.dma('sync', a_o.ap()[q0:q0 + 512, :].rearrange("(s p) d -> p s d", p=128), ob3, [obd], [od], osem)
        else:
            p.dma('sync', a_o.ap()[q0:q0 + qsz, :], ob3[0:qsz, 0, :], [obd], [od], osem)
        p.out_deps.append(od)

    emit_S(0)
    emit_S(1)
    for i in range(len(iters)):
        if i + 2 < len(iters):
            emit_S(i + 2)
        emit_PV(i)
        if i % 7 == 3:
            next(fgen, None)
    for _ in fgen:
        pass
    ar.reset(mk0)

    def al():
        return alias(att_deps)

    xc0 = (ar.f32(T), al())
    xc2 = (ar.f32(T), al())
    mk1 = ar.mark()
    ust = [(ar.f32(T), al(), 'd_u%d' % i) for i in range(2)]
    xc1 = (ar.f32(T), al())
    xcs = [xc0, xc1, xc2]
    for s in range(3):
        u_, ud, usem = ust[s % 2]
        o_, od_ = xcs[s]
        p.dma('sync' if s % 2 == 0 else 'gpsimd', u_, uT.ap()[s], [], [ud], usem)
        hw = hpt[0]
        p.ts('vector', o_, u_, hw[:, s * 3 + 1:s * 3 + 2], hw[:, 9 + s:10 + s], ALU.mult, ALU.add, [ud, hpt[1]], [od_])
        p.stt('vector', o_[:, 1:T], u_[:, 0:T - 1], hw[:, s * 3:s * 3 + 1], o_[:, 1:T], ALU.mult, ALU.add,
              [ud, hpt[1]], [od_])
        p.stt('vector', o_[:, 0:T - 1], u_[:, 1:T], hw[:, s * 3 + 2:s * 3 + 3], o_[:, 0:T - 1], ALU.mult, ALU.add,
              [ud, hpt[1]], [od_])
    p.tt('vector', xc2[0], xc2[0], xc1[0], ALU.mult, [xc1[1]], [xc2[1]])
    ar.reset(mk1)
    rdeps = [ust[0][1], ust[1][1], xc1[1]]

    def al2():
        return alias(rdeps)

    ZT = (ar.bf16(NBLK * 128), al2())
    ZT3 = ZT[0].rearrange("p (s c) -> p s c", c=128)
    G = [(ar.bf16(8320), al2(), 'd_G%d' % i) for i in range(2)]
    YT = (ar.f32(NBLK * 128), al2())
    YT3 = YT[0].rearrange("p (t c) -> p t c", c=128)
    tmpo = [(ar.f32(512), al2()) for _ in range(2)]
    p.memset('gpsimd', ZT[0], 0.0, [ZT[1]])
    for b4 in range(17):
        ps, psd = cx.psum()
        nb = min(4, NBLK - b4 * 4)
        for j in range(nb):
            blk = b4 * 4 + j
            n = min(128, T - blk * 128)
            p.mm(ps[0:n, j * 128:(j + 1) * 128], xc2[0][:, blk * 128:blk * 128 + n], cx.ident, True, True,
                 [xc2[1], cx.cst_d], psd)
        if b4 < 16:
            p.copy(cx.evq(), ZT[0][:, b4 * 512:(b4 + 1) * 512], ps[:, 0:512], [psd], [ZT[1]])
        else:
            p.copy(cx.evq(), ZT[0][0:16, 64 * 128:65 * 128], ps[0:16, 0:128], [psd], [ZT[1]])
    psY = psYd = None
    for c in range(128):
        slot = c % 7
        if slot == 0:
            psY, psYd = cx.psum()
        ga, gad, gas = G[0]
        gb, gbd, gbs = G[1]
        p.dma('sync', ga[:, 0:8320], bass.AP(krev, c * KW, [[1, 128], [1, 8320]]), krds, [gad], gas)
        p.dma('gpsimd', gb[:, 0:8192], bass.AP(krev, c * KW + 8320, [[1, 128], [1, 8192]]), krds, [gbd], gbs)
        for dl in range(0, 65):
            j = 8192 - 128 * dl
            nn = 65 - dl
            p.mm(psY[:, slot * 65 + dl:slot * 65 + dl + nn], ga[:, j:j + 128], ZT3[:, 0:nn, c], dl == 0, False,
                 [gad, ZT[1]], psYd)
        for ad in range(1, 65):
            j = 128 * ad - 128
            nn = 65 - ad
            p.mm(psY[:, slot * 65:slot * 65 + nn], gb[:, j:j + 128], ZT3[:, ad:65, c], False, ad == 64,
                 [gbd, ZT[1]], psYd)
        if slot == 6 or c == 127:
            c0 = c - slot
            nch = slot + 1
            p.copy(cx.evq(), YT3[:, :, c0:c0 + nch].rearrange("p t c -> p c t"),
                   psY[:, 0:nch * 65].rearrange("p (c t) -> p c t", t=65), [psYd], [YT[1]])
    hw = hpt[0]
    for b4 in range(17):
        ps, psd = cx.psum()
        nb = min(4, NBLK - b4 * 4)
        for j in range(nb):
            blk = b4 * 4 + j
            p.mm(ps[:, j * 128:(j + 1) * 128], YT3[:, blk, :], cx.J, True, True, [YT[1], cx.cst_d], psd)
        t0 = b4 * 512
        n = min(512, T - t0)
        tm, tmd = tmpo[b4 % 2]
        p.stt('vector', tm[:, 0:n], xc2[0][:, t0:t0 + n], hw[:, 12:13], ps[:, 0:n], ALU.mult, ALU.add,
              [xc2[1], hpt[1], psd], [tmd])
        p.tt('gpsimd', xc0[0][:, t0:t0 + n], tm[:, 0:n], xc0[0][:, t0:t0 + n], ALU.mult, [tmd], [xc0[1]])
    od = Dep()
    p.dma('sync', y_o.ap(), xc0[0], [xc0[1]], [od], 'd_yo')
    p.out_deps.append(od)
    p.finish()
    return p.build()


def hyena_tables():
    L = T
    t = np.linspace(0.0, 1.0, L, dtype=np.float32)[:, None]
    wpos = (np.float32(2.0 * np.pi) * np.arange(L, dtype=np.float32) / np.float32(L)).astype(np.float32)
    freqs = np.linspace(1e-4, 15, 16, dtype=np.float32)
    ang = (wpos[:, None] * freqs[None, :]).astype(np.float32)
    z = np.concatenate([t, np.cos(ang), -np.sin(ang)], axis=-1).astype(np.float32)
    maxd = np.log(1e-2) / 0.3
    mind = np.log(1e-2) / 1.5
    deltas = np.abs(np.linspace(mind, maxd, HYW, dtype=np.float32))
    decay = np.exp(-t * deltas[None, :]).astype(np.float32)
    zt = np.concatenate([z[::-1].T, z.T], axis=1)
    dec = np.concatenate([decay[::-1].T, decay.T], axis=1)
    return np.ascontiguousarray(zt), np.ascontiguousarray(dec)


def build_C():
    cx = Ctx()
    p, nc, ar = cx.p, cx.nc, cx.ar
    hT = cx.din('hT', [D, TC])
    aT = cx.din('aT', [HYW, TC])
    yT = cx.din('yT', [HYW, TC])
    w_out = cx.din('w_out', [D, D])
    w_gate = cx.din('w_gate', [D, DFF])
    w_up = cx.din('w_up', [D, DFF])
    w_down = cx.din('w_down', [DFF, D])
    gv = cx.din('gv', [128, 32])
    hT_o = cx.dout('hTo', [D, TC])
    h1s = nc.dram_tensor('h1s', [D, TC], F32, kind="Internal")
    cx.load_consts()
    alloc_rms_scratch(cx)
    cx.wi = 0
    g = ar.f32(32)
    gd = Dep()
    p.dma('sync', g, gv.ap(), [], [gd], 'd_g')
    wreg = ar.bf16(16896)
    ws4 = [(wreg[:, i * 4096:(i + 1) * 4096], Dep(), 'd_w%d' % i) for i in range(4)]
    mix = [(ar.bf16(TC), Dep()) for _ in range(16)]
    hb = [(ar.f32(TC), Dep(), 'd_hb%d' % i) for i in range(2)]
    sg = [(ar.f32(342), Dep()) for _ in range(2)]
    mk = ar.mark()
    h = [(ar.f32(TC), Dep()) for _ in range(16)]
    stg = [(ar.f32(TC), Dep()) for _ in range(8)]
    for kc in range(16):
        p.dma('sync', h[kc][0], hT.ap()[kc * 128:(kc + 1) * 128, :], [], [h[kc][1]], 'd_h%d' % kc)
    for grp, src in enumerate((aT, yT)):
        for i in range(8):
            p.dma('gpsimd', stg[i][0], src.ap()[i * 128:(i + 1) * 128, :], [], [stg[i][1]], 'd_s%d' % i)
        for (t0, tn) in CH:
            r, rd = rms_rstd(cx, [(stg[i][0][:, t0:t0 + tn], [stg[i][1]], 128) for i in range(8)], HYW, tn)
            for i in range(8):
                gc = 16 + grp * 8 + i
                p.stt('vector', mix[grp * 8 + i][0][:, t0:t0 + tn], stg[i][0][:, t0:t0 + tn], g[:, gc:gc + 1],
                      r[:, 0:tn], ALU.mult, ALU.mult, [stg[i][1], gd, rd], [mix[grp * 8 + i][1]])

    def evac_out(m0, msz, ci, t0, tn, ps, psd):
        hc, hd = h[m0 // 128]
        p.tt('vector', hc[:, t0:t0 + tn], ps[0:msz, 0:tn], hc[:, t0:t0 + tn], ALU.add, [psd], [hd])

    xs = [(mix[i][0], mix[i][1], 128) for i in range(16)]
    linear(cx, xs, w_out.ap(), D, [(i * 128, 128) for i in range(16)], 256, evac_out, ws4)
    h1d = []
    for kc in range(16):
        d0 = Dep()
        p.dma('sync', h1s.ap()[kc * 128:(kc + 1) * 128, :], h[kc][0], [h[kc][1]], [d0], 'd_sp%d' % kc)
        h1d.append(d0)
    for (t0, tn) in CH:
        r, rd = rms_rstd(cx, [(h[kc][0][:, t0:t0 + tn], [h[kc][1]], 128) for kc in range(16)], D, tn)
        for kc in range(16):
            p.stt('vector', mix[kc][0][:, t0:t0 + tn], h[kc][0][:, t0:t0 + tn], g[:, kc:kc + 1], r[:, 0:tn],
                  ALU.mult, ALU.mult, [h[kc][1], gd, rd], [mix[kc][1]])
    ar.reset(mk)
    olds = [x[1] for x in h] + [x[1] for x in stg]
    act = [(ar.bf16(TC), alias(olds)) for _ in range(DFF // 128)]
    si = 0
    for blk in range(DFF // 256):
        b0 = blk * 256
        sl = []
        for wi_, wsrc in enumerate((w_gate, w_up)):
            wap, wd, wsem = ws4[(blk % 2) * 2 + wi_]
            wv = wap[:, 0:16 * 256].rearrange("p (k m) -> p k m", m=256)
            p.dma('gpsimd', wv, wsrc.ap()[:, b0:b0 + 256].rearrange("(k p) m -> p k m", p=128),
                  [], [wd], wsem)
            sl.append((wv, wd))
        for mc in range(2):
            m = blk * 2 + mc
            for (t0, tn) in CH:
                psg, psgd = cx.psum()
                for kc in range(16):
                    p.mm(psg[:, 0:tn], sl[0][0][:, kc, mc * 128:(mc + 1) * 128], mix[kc][0][:, t0:t0 + tn],
                         kc == 0, kc == 15, [sl[0][1], mix[kc][1]], psgd)
                psu, psud = cx.psum()
                for kc in range(16):
                    p.mm(psu[:, 0:tn], sl[1][0][:, kc, mc * 128:(mc + 1) * 128], mix[kc][0][:, t0:t0 + tn],
                         kc == 0, kc == 15, [sl[1][1], mix[kc][1]], psud)
                s_, sd = sg[si % 2]
                si += 1
                p.act(s_[:, 0:tn], psg[:, 0:tn], AF.Silu, [psgd], [sd])
                p.tt('vector', act[m][0][:, t0:t0 + tn], s_[:, 0:tn], psu[:, 0:tn], ALU.mult, [sd, psud], [act[m][1]])
    wold = [x[1] for x in ws4]
    ws3 = [(wreg[:, i * 5632:(i + 1) * 5632], alias(wold), 'd_w%d' % i) for i in range(3)]

    def evac_down(m0, msz, ci, t0, tn, ps, psd):
        mc = m0 // 128
        hb_, hbd, hsem = hb[mc % 2]
        if ci == 0:
            p.dma('sync', hb_, h1s.ap()[mc * 128:(mc + 1) * 128, :], [h1d[mc]], [hbd], hsem)
        p.tt('vector', hb_[:, t0:t0 + tn], ps[0:msz, 0:tn], hb_[:, t0:t0 + tn], ALU.add, [psd], [hbd])
        if ci == len(CH) - 1:
            od = Dep()
            p.dma('sync', hT_o.ap()[mc * 128:(mc + 1) * 128, :], hb_, [hbd], [od], hsem)
            p.out_deps.append(od)

    xs = [(act[i][0], act[i][1], 128) for i in range(DFF // 128)]
    linear(cx, xs, w_down.ap(), DFF, [(i * 128, 128) for i in range(16)], 128, evac_down, ws3)
    p.finish()
    return p.build()


_PROGS = {}


def _prog(name):
    if name not in _PROGS:
        _PROGS[name] = {'A': build_A, 'B': build_B, 'C': build_C}[name]()
    return _PROGS[name]


def _run(name, in_maps):
    res = run_bass_kernel_spmd(_prog(name), in_maps, core_ids=list(range(NCORE)))
    return res.results


def kernel(x, meta_tokens, norm_mix_g, w_in, q_lat_g, kv_lat_g, w_uq, w_ukv, q_norm_g, k_norm_g,
           conv_w, conv_b, filt_w1, filt_b1, filt_freq1, filt_w2, filt_b2, filt_freq2, filt_w3,
           hy_skip, attn_out_g, hy_out_g, w_out, norm_ffn_g, w_gate, w_up, w_down):
    f32 = np.float32
    ca = np.ascontiguousarray
    x = np.asarray(x, f32)
    h = np.concatenate([np.asarray(meta_tokens, f32), x[0]], axis=0)
    hT = [ca(h[c * TC:(c + 1) * TC].T) for c in range(NCORE)]
    cst = host_consts()
    cos, sin = rope_tables_np()
    cs = [ca(np.concatenate([cos[c * TC:(c + 1) * TC].T, sin[c * TC:(c + 1) * TC].T], axis=1)) for c in range(NCORE)]
    ztab, dtab = hyena_tables()
    for l in range(DEPTH):
        gA = pack_gains_A(np.asarray(norm_mix_g[l], f32), np.asarray(q_lat_g[l], f32), np.asarray(kv_lat_g[l], f32),
                          np.asarray(q_norm_g[l], f32), np.asarray(k_norm_g[l], f32))
        wi, wq, wkv = ca(np.asarray(w_in[l], f32)), ca(np.asarray(w_uq[l], f32)), ca(np.asarray(w_ukv[l], f32))
        rA = _run('A', [{'hT': hT[c], 'w_in': wi, 'w_uq': wq, 'w_ukv': wkv, 'gv': gA, 'cs': cs[c], 'cst': cst}
                        for c in range(NCORE)])
        qT = np.concatenate([r['qT'] for r in rA], axis=2)
        kT = np.concatenate([r['kT'] for r in rA], axis=2)
        v = np.concatenate([r['v'] for r in rA], axis=0)
        uT = np.concatenate([r['uT'] for r in rA], axis=1)
        cw = np.asarray(conv_w[l], f32)
        cb = np.asarray(conv_b[l], f32)
        w3 = np.asarray(filt_w3[l], f32)
        fpp = ca(np.stack([np.asarray(filt_b1[l], f32), np.asarray(filt_freq1[l], f32),
                           np.asarray(filt_b2[l], f32), np.asarray(filt_freq2[l], f32)], axis=1))
        mapsB = []
        for j in range(NCORE):
            ch = slice(j * 128, (j + 1) * 128)
            hp = np.zeros((128, 13), f32)
            for s in range(3):
                for tap in range(3):
                    hp[:, s * 3 + tap] = cw[tap, s * HYW + j * 128:s * HYW + (j + 1) * 128]
                hp[:, 9 + s] = cb[s * HYW + j * 128:s * HYW + (j + 1) * 128]
            hp[:, 12] = np.asarray(hy_skip[l], f32)[ch]
            mapsB.append({'qT': ca(qT[j]), 'kT': ca(kT[j]), 'v': ca(v[:, ch]),
                          'uT': ca(np.stack([uT[s * HYW + j * 128:s * HYW + (j + 1) * 128] for s in range(3)])),
                          'hp': hp, 'fw1': ca(np.asarray(filt_w1[l], f32)), 'fw2': ca(np.asarray(filt_w2[l], f32)),
                          'fw3': ca(np.concatenate([w3[:, ch], w3[:, HYW + j * 128:HYW + (j + 1) * 128]], axis=1)),
                          'fpp': fpp, 'zt': ztab, 'dec': ca(dtab[ch]), 'cst': cst})
        rB = _run('B', mapsB)
        a = np.concatenate([r['a'] for r in rB], axis=1)
        yT = np.concatenate([r['yT'] for r in rB], axis=0)
        gC = np.zeros((128, 32), f32)
        gC[:, 0:16] = np.asarray(norm_ffn_g[l], f32).reshape(16, 128).T
        gC[:, 16:24] = np.asarray(attn_out_g[l], f32).reshape(8, 128).T
        gC[:, 24:32] = np.asarray(hy_out_g[l], f32).reshape(8, 128).T
        wo, wg, wu, wd = (ca(np.asarray(w_out[l], f32)), ca(np.asarray(w_gate[l], f32)),
                          ca(np.asarray(w_up[l], f32)), ca(np.asarray(w_down[l], f32)))
        rC = _run('C', [{'hT': hT[c], 'aT': ca(a[c * TC:(c + 1) * TC].T), 'yT': ca(yT[:, c * TC:(c + 1) * TC]),
                         'w_out': wo, 'w_gate': wg, 'w_up': wu, 'w_down': wd, 'gv': gC, 'cst': cst}
                        for c in range(NCORE)])
        hT = [r['hTo'] for r in rC]
    hfull = np.concatenate([t.T for t in hT], axis=0)
    return ca(hfull[NMETA:][None].astype(f32))
```
